# Optimizing a Trainium2 kernel written in Bass

```python
import jax, jax.numpy as jnp
from jax import lax
import numpy as np

D_MODEL = 1024
BATCH = 16
SEQ = 2048
DEPTH = 2

GLA_HEADS = 4
GLA_DK = D_MODEL // 16
GLA_DV = D_MODEL // 8
GLA_RANK = 16
GLA_TAU = 16.0
GLA_CHUNK = 64
CONV_DIM = D_MODEL // 2
CONV_WIDTH = 3
SWA_HEADS = 16
SWA_KV_HEADS = 4
SWA_HEAD_DIM = D_MODEL // SWA_HEADS
WINDOW = 128
SWA_BLOCK = WINDOW
ROT_DIM = SWA_HEAD_DIM // 4
ROPE_THETA = 500000.0
XA_HEADS = 4
XA_HEAD_DIM = D_MODEL // XA_HEADS
MEM_LEN = 256
FFN_HIDDEN = -(-8 * D_MODEL // (3 * 256)) * 256

HYB_IN_WIDTH = 2 * GLA_HEADS * GLA_DK + 2 * GLA_HEADS * GLA_DV + GLA_RANK + 3 * CONV_DIM
HYB_OUT_WIDTH = GLA_HEADS * GLA_DV + CONV_DIM
SWA_QKV_WIDTH = (SWA_HEADS + 2 * SWA_KV_HEADS) * SWA_HEAD_DIM
N_EVEN = (DEPTH + 1) // 2
N_ODD = DEPTH // 2
RMS_EPS = 1e-6

kernel_name = 'hybrid_gla_shortconv_swa_sink_trunk'


def rms_norm(x, gain):
    xf = x.astype(jnp.float32)
    y = xf * lax.rsqrt(jnp.mean(xf * xf, axis=-1, keepdims=True) + RMS_EPS)
    return (y * gain.astype(jnp.float32)).astype(x.dtype)


def rope_tables(positions):
    inv_freq = ROPE_THETA ** (-jnp.arange(0, ROT_DIM, 2, dtype=jnp.float32) / ROT_DIM)
    ang = positions.astype(jnp.float32)[..., None] * inv_freq
    return jnp.cos(ang), jnp.sin(ang)


def rope_partial(x, cos, sin):
    half = ROT_DIM // 2
    x1 = x[..., :half].astype(jnp.float32)
    x2 = x[..., half:ROT_DIM].astype(jnp.float32)
    c = cos[:, :, None, :]
    s = sin[:, :, None, :]
    rot = jnp.concatenate([x1 * c - x2 * s, x2 * c + x1 * s], axis=-1).astype(x.dtype)
    return jnp.concatenate([rot, x[..., ROT_DIM:]], axis=-1)


def gla_chunked(q, k, v, log_a):
    out_dtype = v.dtype
    bsz, t, h, dk = q.shape
    dv = v.shape[-1]
    n = t // GLA_CHUNK
    f32 = jnp.float32
    q, k, v, log_a = (arr.astype(f32).reshape(bsz, n, GLA_CHUNK, h, arr.shape[-1])
                      for arr in (q, k, v, log_a))
    b = jnp.cumsum(log_a, axis=2)
    b_last = b[:, :, -1]
    q_in = q * jnp.exp(b)
    k_in = k * jnp.exp(-b)
    k_out = k * jnp.exp(b_last[:, :, None] - b)
    causal = jnp.tril(jnp.ones((GLA_CHUNK, GLA_CHUNK), dtype=bool))
    scores = jnp.einsum('bnihd,bnjhd->bnhij', q_in, k_in)
    scores = jnp.where(causal, scores, 0.0)
    o_intra = jnp.einsum('bnhij,bnjhv->bnihv', scores, v)
    kv = jnp.einsum('bnjhd,bnjhv->nbhdv', k_out, v)
    decay = jnp.exp(b_last).transpose(1, 0, 2, 3)

    def step(state, inp):
        kv_c, d_c = inp
        return d_c[..., None] * state + kv_c, state

    _, s_prev = lax.scan(step, jnp.zeros((bsz, h, dk, dv), f32), (kv, decay))
    o_inter = jnp.einsum('bnihd,nbhdv->bnihv', q_in, s_prev)
    return (o_intra + o_inter).reshape(bsz, t, h, dv).astype(out_dtype)


def gla_shortconv_mixer(h, w_in, w_gate2, gate_bias, gla_out_gain, conv_w, w_out):
    bsz, t, _ = h.shape
    widths = [GLA_HEADS * GLA_DK, GLA_HEADS * GLA_DK, GLA_HEADS * GLA_DV, GLA_RANK,
              GLA_HEADS * GLA_DV, CONV_DIM, CONV_DIM, CONV_DIM]
    cuts = [int(c) for c in np.cumsum(widths)[:-1]]
    q, k, v, g_low, g_out, conv_b, conv_c, conv_in = jnp.split(h @ w_in, cuts, axis=-1)
    q = q.reshape(bsz, t, GLA_HEADS, GLA_DK) * (GLA_DK ** -0.5)
    k = k.reshape(bsz, t, GLA_HEADS, GLA_DK)
    v = v.reshape(bsz, t, GLA_HEADS, GLA_DV)
    gate_logits = (g_low @ w_gate2 + gate_bias).astype(jnp.float32)
    log_a = (jax.nn.log_sigmoid(gate_logits) / GLA_TAU).reshape(bsz, t, GLA_HEADS, GLA_DK)
    o = gla_chunked(q, k, v, log_a)
    o = rms_norm(o, gla_out_gain) * jax.nn.silu(g_out).reshape(bsz, t, GLA_HEADS, GLA_DV)
    y_gla = o.reshape(bsz, t, GLA_HEADS * GLA_DV)
    u = conv_c * conv_in
    u = lax.conv_general_dilated(u, conv_w[:, None, :], window_strides=(1,),
                                 padding=[(CONV_WIDTH - 1, 0)],
                                 dimension_numbers=('NWC', 'WIO', 'NWC'),
                                 feature_group_count=CONV_DIM)
    y_conv = conv_b * u
    return jnp.concatenate([y_gla, y_conv], axis=-1) @ w_out


def banded_sink_attention(q, k, v, sinks):
    bsz, t, hq, hd = q.shape
    hkv = k.shape[2]
    g = hq // hkv
    w = SWA_BLOCK
    n = t // w
    qb = q.reshape(bsz, n, w, hkv, g, hd).transpose(1, 0, 2, 3, 4, 5)
    kb = k.reshape(bsz, n, w, hkv, hd).transpose(1, 0, 2, 3, 4)
    vb = v.reshape(bsz, n, w, hkv, hd).transpose(1, 0, 2, 3, 4)
    kk = jnp.concatenate([jnp.concatenate([jnp.zeros_like(kb[:1]), kb[:-1]], axis=0), kb], axis=2)
    vv = jnp.concatenate([jnp.concatenate([jnp.zeros_like(vb[:1]), vb[:-1]], axis=0), vb], axis=2)
    qi = jnp.arange(w)[:, None] + w
    si = jnp.arange(2 * w)[None, :]
    band = (qi - si >= 0) & (qi - si < WINDOW)
    sink = sinks.astype(jnp.float32).reshape(hkv, g)[None, :, :, None, None]
    scale = hd ** -0.5

    def block(args):
        qn, kn, vn, idx = args
        s = jnp.einsum('bqkgd,bskd->bkgqs', qn, kn).astype(jnp.float32) * scale
        valid = band & ((si >= w) | (idx > 0))
        s = jnp.where(valid, s, -jnp.inf)
        m = jnp.maximum(jnp.max(s, axis=-1, keepdims=True), sink)
        p = jnp.exp(s - m)
        p = p / (jnp.sum(p, axis=-1, keepdims=True) + jnp.exp(sink - m))
        return jnp.einsum('bkgqs,bskd->bqkgd', p.astype(vn.dtype), vn)

    o = lax.map(block, (qb, kk, vv, jnp.arange(n)))
    return o.transpose(1, 0, 2, 3, 4, 5).reshape(bsz, t, hq, hd)


def swa_sink_mixer(h, cos, sin, w_qkv, q_gain, k_gain, sinks, w_out):
    bsz, t, _ = h.shape
    cq = SWA_HEADS * SWA_HEAD_DIM
    ck = cq + SWA_KV_HEADS * SWA_HEAD_DIM
    q, k, v = jnp.split(h @ w_qkv, [cq, ck], axis=-1)
    q = q.reshape(bsz, t, SWA_HEADS, SWA_HEAD_DIM)
    k = k.reshape(bsz, t, SWA_KV_HEADS, SWA_HEAD_DIM)
    v = v.reshape(bsz, t, SWA_KV_HEADS, SWA_HEAD_DIM)
    q = rope_partial(rms_norm(q, q_gain), cos, sin)
    k = rope_partial(rms_norm(k, k_gain), cos, sin)
    o = banded_sink_attention(q, k, v, sinks)
    return o.reshape(bsz, t, cq) @ w_out


def memory_cross_attention(h, mem_h, wq, wkv, q_gain, k_gain, wo):
    bsz, t, d = h.shape
    m = mem_h.shape[1]
    q = (h @ wq).reshape(bsz, t, XA_HEADS, XA_HEAD_DIM)
    k, v = jnp.split(mem_h @ wkv, 2, axis=-1)
    k = k.reshape(bsz, m, XA_HEADS, XA_HEAD_DIM)
    v = v.reshape(bsz, m, XA_HEADS, XA_HEAD_DIM)
    q = rms_norm(q, q_gain)
    k = rms_norm(k, k_gain)
    s = jnp.einsum('bthd,bmhd->bhtm', q, k).astype(jnp.float32) * (XA_HEAD_DIM ** -0.5)
    p = jax.nn.softmax(s, axis=-1)
    o = jnp.einsum('bhtm,bmhd->bthd', p.astype(v.dtype), v)
    return o.reshape(bsz, t, d) @ wo


def swiglu(h, w_gate_up, w_down):
    gate, up = jnp.split(h @ w_gate_up, 2, axis=-1)
    return (jax.nn.silu(gate) * up) @ w_down


def setup_inputs(seed: int = 0) -> dict:
    key = jax.random.key(seed)
    ks = iter(jax.random.split(key, 32))
    f32 = jnp.float32

    def dense(shape, fan_in):
        return jax.random.normal(next(ks), shape, f32) * fan_in ** -0.5

    def gain(shape):
        return 1.0 + 0.02 * jax.random.normal(next(ks), shape, f32)

    x = jax.random.normal(next(ks), (BATCH, SEQ, D_MODEL), f32)
    mem = jax.random.normal(next(ks), (BATCH, MEM_LEN, D_MODEL), f32)
    offsets = jax.random.randint(next(ks), (BATCH, 1), 0, 1024, dtype=jnp.int32)
    positions = offsets + jnp.arange(SEQ, dtype=jnp.int32)[None, :]
    return {
        'x': x,
        'mem': mem,
        'positions': positions,
        'mix_norm': gain((DEPTH, D_MODEL)),
        'hyb_w_in': dense((N_EVEN, D_MODEL, HYB_IN_WIDTH), D_MODEL),
        'gla_w_gate2': dense((N_EVEN, GLA_RANK, GLA_HEADS * GLA_DK), GLA_RANK),
        'gla_gate_bias': 0.1 * jax.random.normal(next(ks), (N_EVEN, GLA_HEADS * GLA_DK), f32),
        'gla_out_gain': gain((N_EVEN, GLA_DV)),
        'conv_w': dense((N_EVEN, CONV_WIDTH, CONV_DIM), CONV_WIDTH),
        'hyb_w_out': dense((N_EVEN, HYB_OUT_WIDTH, D_MODEL), HYB_OUT_WIDTH),
        'swa_w_qkv': dense((N_ODD, D_MODEL, SWA_QKV_WIDTH), D_MODEL),
        'swa_q_gain': gain((N_ODD, SWA_HEAD_DIM)),
        'swa_k_gain': gain((N_ODD, SWA_HEAD_DIM)),
        'swa_sinks': jax.random.normal(next(ks), (N_ODD, SWA_HEADS), f32),
        'swa_w_out': dense((N_ODD, SWA_HEADS * SWA_HEAD_DIM, D_MODEL), SWA_HEADS * SWA_HEAD_DIM),
        'mem_norm': gain((DEPTH, D_MODEL)),
        'xa_norm': gain((DEPTH, D_MODEL)),
        'xa_wq': dense((DEPTH, D_MODEL, D_MODEL), D_MODEL),
        'xa_wkv': dense((DEPTH, D_MODEL, 2 * D_MODEL), D_MODEL),
        'xa_q_gain': gain((DEPTH, XA_HEAD_DIM)),
        'xa_k_gain': gain((DEPTH, XA_HEAD_DIM)),
        'xa_wo': dense((DEPTH, D_MODEL, D_MODEL), D_MODEL),
        'ffn_norm': gain((DEPTH, D_MODEL)),
        'ffn_w_gate_up': dense((DEPTH, D_MODEL, 2 * FFN_HIDDEN), D_MODEL),
        'ffn_w_down': dense((DEPTH, FFN_HIDDEN, D_MODEL), FFN_HIDDEN),
    }


def reference(x, mem, positions, mix_norm, hyb_w_in, gla_w_gate2, gla_gate_bias, gla_out_gain,
              conv_w, hyb_w_out, swa_w_qkv, swa_q_gain, swa_k_gain, swa_sinks, swa_w_out,
              mem_norm, xa_norm, xa_wq, xa_wkv, xa_q_gain, xa_k_gain, xa_wo,
              ffn_norm, ffn_w_gate_up, ffn_w_down):
    cos, sin = rope_tables(positions)
    for layer in range(DEPTH):
        i = layer // 2
        h = rms_norm(x, mix_norm[layer])
        if layer % 2 == 0:
            x = x + gla_shortconv_mixer(h, hyb_w_in[i], gla_w_gate2[i], gla_gate_bias[i],
                                        gla_out_gain[i], conv_w[i], hyb_w_out[i])
        else:
            x = x + swa_sink_mixer(h, cos, sin, swa_w_qkv[i], swa_q_gain[i], swa_k_gain[i],
                                   swa_sinks[i], swa_w_out[i])
        mem_h = rms_norm(mem, mem_norm[layer])
        x = x + memory_cross_attention(rms_norm(x, xa_norm[layer]), mem_h, xa_wq[layer],
                                       xa_wkv[layer], xa_q_gain[layer], xa_k_gain[layer],
                                       xa_wo[layer])
        x = x + swiglu(rms_norm(x, ffn_norm[layer]), ffn_w_gate_up[layer], ffn_w_down[layer])
    return x
```

```python
import concourse.bass as bass
import concourse.mybir as mybir

F32 = mybir.dt.float32
BF16 = mybir.dt.bfloat16
I32 = mybir.dt.int32
ALU = mybir.AluOpType
AF = mybir.ActivationFunctionType
AX = mybir.AxisListType

CELL = 512


def _dsize(dt):
    s = str(dt)
    if "64" in s:
        return 8
    if "32" in s:
        return 4
    if "16" in s:
        return 2
    return 1


def ap_cells(ap):
    sp = str(ap.space)
    if "SB" not in sp and "PSUM" not in sp:
        return None
    ds = _dsize(ap.dtype)
    pat = ap.ap
    row = pat[0][0]
    off = ap.offset % row if row > 0 else ap.offset
    ext = 1
    for st, cn in pat[1:]:
        ext += abs(st) * (cn - 1)
    b0 = off * ds
    b1 = (off + ext) * ds
    name = ap.name
    cell = 2048 if "PSUM" in sp else CELL
    return [(name, c) for c in range(b0 // cell, (b1 - 1) // cell + 1)]


class Prog:
    ENGS = ("pe", "act", "dve", "pool", "sp")

    def __init__(self, nc, same_engine_sync=True):
        self.nc = nc
        self.q = {e: [] for e in self.ENGS}
        self.cnt = {e: 0 for e in self.ENGS}
        self.sems = {}
        self.seen = {e: {} for e in self.ENGS}
        self.cells = {}
        self.dma_cnt = {}
        self.same_engine_sync = same_engine_sync
        self.nwaits = 0
        self.nops = 0

    def add_sem(self, key, handle):
        self.sems[key] = handle

    def _deps(self, engine, reads, writes):
        deps = {}

        def add(tok):
            if tok is None:
                return
            k, v = tok
            if k == engine and (engine == "pe" or not self.same_engine_sync):
                return
            if deps.get(k, 0) < v:
                deps[k] = v

        rc, wc = [], []
        for ap in reads:
            c = ap_cells(ap)
            if c:
                rc += c
        for ap in writes:
            c = ap_cells(ap)
            if c:
                wc += c
        for c in rc:
            st = self.cells.get(c)
            if st:
                add(st[0])
        for c in wc:
            st = self.cells.get(c)
            if st:
                add(st[0])
                for tok in st[1].items():
                    add(tok)
        seen = self.seen[engine]
        waits = []
        for k, v in deps.items():
            if seen.get(k, 0) < v:
                seen[k] = v
                waits.append((k, v))
        return waits, rc, wc

    def _commit(self, tok, rc, wc):
        for c in rc:
            st = self.cells.get(c)
            if st is None:
                st = [None, {}]
                self.cells[c] = st
            k, v = tok
            if st[1].get(k, 0) < v:
                st[1][k] = v
        for c in wc:
            self.cells[c] = [tok, {}]

    def op(self, engine, fn, reads, writes):
        waits, rc, wc = self._deps(engine, reads, writes)
        self.cnt[engine] += 1
        tok = (engine, self.cnt[engine])
        self._commit(tok, rc, wc)
        self.q[engine].append((waits, fn, engine, 1))
        self.nwaits += len(waits)
        self.nops += 1

    def dma(self, queue, out, in_, semkey, group_final=None, **kw):
        waits, rc, wc = self._deps(queue, [in_], [out])
        self.dma_cnt[semkey] = self.dma_cnt.get(semkey, 0) + 16
        tok = (semkey, self.dma_cnt[semkey] + 16 * ((group_final or 1) - 1))
        self._commit(tok, rc, wc)

        def fn(eng, out=out, in_=in_, kw=kw):
            return eng.dma_start(out=out, in_=in_, **kw)

        self.q[queue].append((waits, fn, semkey, 16))
        self.nwaits += len(waits)

    def wait_all_dma(self, engine, keys):
        waits = []
        for k in keys:
            v = self.dma_cnt.get(k, 0)
            if v and self.seen[engine].get(k, 0) < v:
                self.seen[engine][k] = v
                waits.append((k, v))
        self.q[engine].append((waits, None, None, 0))

    def barrier(self, engines=("pe", "act", "dve")):
        for e in engines:
            waits = []
            for f in engines:
                if f == e:
                    continue
                v = self.cnt[f]
                if v and self.seen[e].get(f, 0) < v:
                    self.seen[e][f] = v
                    waits.append((f, v))
            if waits:
                self.q[e].append((waits, None, None, 0))

    def act(self, out, in_, func, bias=0.0, scale=1.0, accum_out=None):
        reads = [in_]
        if not isinstance(bias, (int, float)):
            reads.append(bias)
        if not isinstance(scale, (int, float)):
            reads.append(scale)
        writes = [out] + ([accum_out] if accum_out is not None else [])

        def fn(eng):
            kw = {}
            if accum_out is not None:
                kw["accum_out"] = accum_out
            return eng.activation(out=out, in_=in_, func=func, bias=bias, scale=scale, **kw)

        self.op("act", fn, reads, writes)

    def tt(self, eng_name, out, in0, in1, op):
        def fn(eng):
            return eng.tensor_tensor(out=out, in0=in0, in1=in1, op=op)

        self.op(eng_name, fn, [in0, in1], [out])

    def ts(self, eng_name, out, in0, s1, op0, s2=None, op1=None):
        reads = [in0]
        if not isinstance(s1, (int, float)):
            reads.append(s1)
        if s2 is not None and not isinstance(s2, (int, float)):
            reads.append(s2)

        def fn(eng):
            if op1 is None:
                return eng.tensor_single_scalar(out=out, in_=in0, scalar=s1, op=op0)
            return eng.tensor_scalar(out=out, in0=in0, scalar1=s1, scalar2=s2, op0=op0, op1=op1)

        self.op(eng_name, fn, reads, [out])

    def stt(self, out, in0, scalar, in1, op0, op1, eng_name="dve"):
        reads = [in0, in1]
        if not isinstance(scalar, (int, float)):
            reads.append(scalar)

        def fn(eng):
            return eng.scalar_tensor_tensor(out=out, in0=in0, scalar=scalar, in1=in1, op0=op0, op1=op1)

        self.op(eng_name, fn, reads, [out])

    def copy(self, eng_name, out, in_):
        if eng_name == "act":
            def fn(eng):
                return eng.copy(out=out, in_=in_)
        else:
            def fn(eng):
                return eng.tensor_copy(out=out, in_=in_)
        self.op(eng_name, fn, [in_], [out])

    def memset(self, eng_name, out, val):
        def fn(eng):
            return eng.memset(out, val)

        self.op(eng_name, fn, [], [out])

    def scan(self, out, d0, d1, initial, op0, op1):
        reads = [d0, d1]
        if not isinstance(initial, (int, float)):
            reads.append(initial)

        def fn(eng):
            return eng.tensor_tensor_scan(out=out, data0=d0, data1=d1, initial=initial, op0=op0, op1=op1)

        self.op("dve", fn, reads, [out])

    def mm(self, out, pairs, extra_reads=()):
        reads = list(extra_reads)
        for l, r in pairs:
            reads += [l, r]
        n = len(pairs)

        def fn(eng):
            ins = None
            for i, (l, r) in enumerate(pairs):
                ins = eng.matmul(out, l, r, start=(i == 0), stop=(i == n - 1))
            return ins

        self.op("pe", fn, reads, [out])

    def mm_multi(self, groups):
        reads, writes = [], []
        for out, pairs in groups:
            writes.append(out)
            for l, r in pairs:
                reads += [l, r]

        def fn(eng):
            ins = None
            for out, pairs in groups:
                n = len(pairs)
                for i, (l, r) in enumerate(pairs):
                    ins = eng.matmul(out, l, r, start=(i == 0), stop=(i == n - 1))
            return ins

        self.op("pe", fn, reads, writes)

    def emit(self):
        nc = self.nc
        sems = self.sems
        q = self.q

        def run(eng, name):
            for waits, fn, sigkey, inc in q[name]:
                for k, v in waits:
                    eng.wait_ge(sems[k], v)
                if fn is None:
                    continue
                ins = fn(eng)
                if sigkey is not None:
                    ins.then_inc(sems[sigkey], inc)

        with nc.Block() as block:
            @block.tensor
            def _(eng):
                run(eng, "pe")

            @block.scalar
            def _(eng):
                run(eng, "act")

            @block.vector
            def _(eng):
                run(eng, "dve")

            @block.gpsimd
            def _(eng):
                run(eng, "pool")

            @block.sync
            def _(eng):
                run(eng, "sp")

import numpy as np
from contextlib import ExitStack
from concourse.bass_utils import run_bass_kernel_spmd

NCORES = 8
SAME_ENGINE_SYNC = True
FLAGS = ''
D = 1024
KC = 8
TT = 512
FH = 2816
SLOT = 8704
NSLOT = 3
EPS = 1e-6
MEM = 256
TWO_PI = 6.283185307179586
PI = 3.141592653589793

SM_MIX, SM_XAN, SM_FFN, SM_MEMN = 0, 16, 32, 48
SM_GBIAS, SM_OGAIN, SM_CONVW, SM_SQG, SM_SKG, SM_SINK = 64, 66, 67, 79, 80, 81
SM_XQG, SM_XKG, SM_INVF, NSM = 97, 101, 105, 106

FFN_GROUPS = [(0, 4), (4, 4), (8, 4), (12, 4), (16, 4), (20, 2)]


def _fm(w):
    k, n = w.shape
    return np.ascontiguousarray(w.reshape(k // 128, 128, n).transpose(1, 0, 2)).reshape(128, -1)


def build_pieces(inp):
    pieces = []
    for l in range(2):
        if l == 0:
            w_in = inp["hyb_w_in"][0]
            w_out = inp["hyb_w_out"][0]
            wg2 = inp["gla_w_gate2"][0]
            for c in range(2):
                cols = np.concatenate([w_in[:, 128 * c:128 * c + 128], w_in[:, 256 + 128 * c:256 + 128 * c + 128],
                                       w_in[:, 512 + 256 * c:512 + 256 * c + 256], w_in[:, 1024:1040],
                                       w_in[:, 1040 + 256 * c:1040 + 256 * c + 256]], axis=1)
                g2 = np.zeros((128, 128), np.float32)
                g2[0:16, :] = wg2[:, 128 * c:128 * c + 128]
                pieces.append(("gla%d" % c, np.concatenate([_fm(cols), _fm(w_out[256 * c:256 * c + 256, :]), g2], axis=1)))
            for c in range(2):
                cols = np.concatenate([w_in[:, 1552 + 256 * c:1552 + 256 * c + 256], w_in[:, 2064 + 256 * c:2064 + 256 * c + 256],
                                       w_in[:, 2576 + 256 * c:2576 + 256 * c + 256]], axis=1)
                pieces.append(("conv%d" % c, np.concatenate([_fm(cols), _fm(w_out[512 + 256 * c:512 + 256 * c + 256, :])], axis=1)))
        else:
            w_qkv = inp["swa_w_qkv"][0]
            w_out = inp["swa_w_out"][0]
            for g in range(4):
                kc = w_qkv[:, 1024 + 64 * g:1024 + 64 * g + 64]
                vc = w_qkv[:, 1280 + 64 * g:1280 + 64 * g + 64]
                cols = np.concatenate([w_qkv[:, 256 * g:256 * g + 256], kc, kc, vc, vc], axis=1)
                pieces.append(("swa%d" % g, np.concatenate([_fm(cols), _fm(w_out[256 * g:256 * g + 256, :])], axis=1)))
        wkv = inp["xa_wkv"][l]
        pieces.append(("xk", _fm(wkv[:, 0:1024])))
        pieces.append(("xv", _fm(wkv[:, 1024:2048])))
        wq = inp["xa_wq"][l]
        wo = inp["xa_wo"][l]
        for j in range(2):
            pieces.append(("xq%d" % j, np.concatenate([_fm(wq[:, 512 * j:512 * j + 512]), _fm(wo[512 * j:512 * j + 512, :])], axis=1)))
        wgu = inp["ffn_w_gate_up"][l]
        wd = inp["ffn_w_down"][l]
        for (c0, n) in FFN_GROUPS:
            cols = np.concatenate([wgu[:, 128 * c0:128 * (c0 + n)], wgu[:, FH + 128 * c0:FH + 128 * (c0 + n)]], axis=1)
            pieces.append(("fgu", _fm(cols)))
            pieces.append(("fd", _fm(wd[128 * c0:128 * (c0 + n), :])))
    return pieces


def build_small(inp):
    sm = np.zeros((128, NSM), np.float32)
    p = np.arange(128)
    for l in range(2):
        sm[:, SM_MIX + 8 * l:SM_MIX + 8 * l + 8] = inp["mix_norm"][l].reshape(8, 128).T
        sm[:, SM_XAN + 8 * l:SM_XAN + 8 * l + 8] = inp["xa_norm"][l].reshape(8, 128).T
        sm[:, SM_FFN + 8 * l:SM_FFN + 8 * l + 8] = inp["ffn_norm"][l].reshape(8, 128).T
        sm[:, SM_MEMN + 8 * l:SM_MEMN + 8 * l + 8] = inp["mem_norm"][l].reshape(8, 128).T
        sm[:, SM_XQG + 2 * l:SM_XQG + 2 * l + 2] = inp["xa_q_gain"][l].reshape(2, 128).T
        sm[:, SM_XKG + 2 * l:SM_XKG + 2 * l + 2] = inp["xa_k_gain"][l].reshape(2, 128).T
    sm[:, SM_GBIAS:SM_GBIAS + 2] = inp["gla_gate_bias"][0].reshape(2, 128).T
    sm[:, SM_OGAIN] = inp["gla_out_gain"][0]
    cw = inp["conv_w"][0]
    for j in range(4):
        for tap in range(3):
            sm[:, SM_CONVW + 3 * j + tap] = cw[tap, 128 * j:128 * j + 128]
    sm[:, SM_SQG] = inp["swa_q_gain"][0][p % 64]
    sm[:, SM_SKG] = inp["swa_k_gain"][0][p % 64]
    sm[:, SM_SINK:SM_SINK + 16] = inp["swa_sinks"][0][None, :]
    d = p % 64
    invf = np.zeros(128, np.float32)
    theta = np.float32(500000.0)
    fr = (theta ** (-np.arange(0, 16, 2, dtype=np.float32) / np.float32(16))).astype(np.float32)
    invf[d < 16] = fr[d[d < 16] % 8]
    sm[:, SM_INVF] = invf
    return sm


def build_cmat():
    cm = np.zeros((128, 6, 128), np.float32)
    j = np.arange(128)[:, None]
    i = np.arange(128)[None, :]
    cm[:, 0, :] = (i == j)
    cm[:, 1, :] = (i >= j)
    cm[:, 2, :] = (j > i)
    pm = np.zeros((128, 128), np.float32)
    for base in (0, 64):
        for dd in range(8):
            pm[base + dd, base + dd + 8] = -1.0
            pm[base + dd + 8, base + dd] = 1.0
    cm[:, 3, :] = pm.T
    cm[:, 4, :] = np.where(i >= j, 0.0, -30000.0)
    cm[:, 5, :] = np.where(j > i, 0.0, -30000.0)
    return cm.reshape(128, 768)


class Arena:
    def __init__(self, t, n):
        self.t, self.n, self.o = t, n, 0

    def reset(self):
        self.o = 0

    def take(self, n):
        assert self.o + n <= self.n, (self.o, n, self.n)
        v = self.t[:, self.o:self.o + n]
        self.o += n
        return v


def build_program(T, NSEQ, layers=(0, 1), dbg=False, maxstage=10 ** 9):
    NT = T // TT
    NB = T // 128
    nc = bass.Bass("TRN2", target_bir_lowering=False)
    keep = [i for i in range(len(PIECE_SIZES)) if PIECE_LAYER[i] in layers]
    piece_sizes = [PIECE_SIZES[i] for i in keep]
    WTOT = sum(piece_sizes)
    xT = nc.dram_tensor("xT", [NSEQ, 128, 8 * T], F32, kind="ExternalInput").ap()
    memT = nc.dram_tensor("memT", [NSEQ, 128, 8 * MEM], F32, kind="ExternalInput").ap()
    posd = nc.dram_tensor("pos", [NSEQ, 1, T], I32, kind="ExternalInput").ap()
    wbig = nc.dram_tensor("wbig", [128, WTOT], F32, kind="ExternalInput").ap()
    smalld = nc.dram_tensor("small", [128, NSM], F32, kind="ExternalInput").ap()
    cmatd = nc.dram_tensor("cmat", [128, 768], F32, kind="ExternalInput").ap()
    yT = nc.dram_tensor("yT", [NSEQ, 128, 8 * T], F32, kind="ExternalOutput").ap()
    NDBG = 6
    if dbg:
        dbgd = nc.dram_tensor("dbg", [NDBG, 128, 8 * T], F32, kind="ExternalOutput").ap()

    es = ExitStack()
    with es:
        def sb(name, shape, dt):
            return es.enter_context(nc.sbuf_tensor(name, shape, dt))

        X = sb("X", [128, 8, T], F32)
        H = sb("H", [128, 8, T], BF16)
        W = sb("W", [128, NSLOT, SLOT], BF16)
        STG = sb("STG", [128, 2, 1024], F32)
        cm = sb("cm", [128, 6, 128], BF16)
        ones = sb("ones", [128, 128], BF16)
        bd = sb("bd", [128, 128], BF16)
        sm = sb("sm", [128, NSM], F32)
        negb = sb("negb", [128, 2], F32)
        esink = sb("esink", [128, 16], F32)
        msk = sb("msk", [128, 512], F32)
        posi = sb("posi", [128, 512], I32)
        nsq = sb("nsq", [128, 2, 512], BF16)
        nt1 = sb("nt1", [128, 512], F32)
        nrstd = sb("nrstd", [128, 512], F32)
        NSF, NSB = 4608, 11264
        SFt = sb("SF", [128, NSF], F32)
        SBt = sb("SB", [128, NSB], BF16)
        ps = es.enter_context(nc.psum_tensor("ps", [128, 8, 512], F32))
        P = Prog(nc, same_engine_sync=SAME_ENGINE_SYNC)
        for e in ("pe", "act", "dve", "pool"):
            P.add_sem(e, es.enter_context(nc.semaphore("s_" + e)))
        dkeys = ["ldx", "ldm", "ldp", "ldc", "st", "dbg"] + ["lx%d" % i for i in range(8)] + ["sx%d" % i for i in range(8)] + ["w%d" % i for i in range(NSLOT)]
        for k in dkeys:
            P.add_sem(k, es.enter_context(nc.semaphore("d_" + k)))
        SF = Arena(SFt, NSF)
        SB = Arena(SBt, NSB)

        ident, tri, trip, pmT = cm[:, 0, :], cm[:, 1, :], cm[:, 2, :], cm[:, 3, :]

        def bank(b):
            return ps[:, b, :]

        def bank2(b):
            return ps[:, b:b + 2, :]

        offs = np.cumsum([0] + piece_sizes).tolist()
        npieces_pass = len(piece_sizes)
        wst = {"loaded": 0, "cur": 0, "stg": 0}
        total_pieces = npieces_pass * NSEQ
        def piece_list():
            lst = []
            for s in range(NSEQ):
                for i in range(npieces_pass):
                    lst.append(i)
            return lst

        plist = piece_list()

        def wload_upto(n):
            while wst["loaded"] < min(n, len(plist)):
                i = wst["loaded"]
                pi_ = plist[i]
                slot = i % NSLOT
                size = piece_sizes[pi_]
                o0 = 0
                while o0 < size:
                    n_ = min(1024, size - o0)
                    j = wst["stg"] % 2
                    wst["stg"] += 1
                    P.dma("sp", STG[:, j, 0:n_], wbig[:, offs[pi_] + o0:offs[pi_] + o0 + n_], "w%d" % j)
                    P.copy("pool", W[:, slot, o0:o0 + n_], STG[:, j, 0:n_])
                    o0 += n_
                wst["loaded"] += 1

        def wacquire(k=1):
            i = wst["cur"]
            wload_upto(i + k)
            views = [W[:, (i + j) % NSLOT, :] for j in range(k)]
            wst["cur"] += k
            return views

        def wrelease():
            wload_upto(wst["cur"] + NSLOT)

        P.dma("sp", sm[:], smalld, "ldc", group_final=4)
        cmf = SF.take(768)
        P.dma("sp", cmf, cmatd, "ldc", group_final=3)
        P.dma("sp", posi[:, 0:16], posd[0, :, 0:16].partition_broadcast(128), "ldc", group_final=2)
        P.dma("sp", nt1[:, 0:16], memT[0, :, 0:16], "ldc", group_final=1)
        P.copy("dve", cm[:].rearrange("p a b -> p (a b)"), cmf)
        P.memset("dve", ones[:], 1.0)
        P.memset("dve", bd[:], 0.0)
        P.memset("dve", bd[0:64, 0:64], 1.0)
        P.memset("dve", bd[64:128, 64:128], 1.0)
        P.memset("dve", msk[:], 1.0)
        P.memset("dve", msk[:].rearrange("p (c j) -> p c j", j=128)[:, :, 0:1], 0.0)
        P.ts("dve", negb[:], sm[:, SM_GBIAS:SM_GBIAS + 2], -1.0, ALU.mult)
        P.act(esink[:], sm[:, SM_SINK:SM_SINK + 16], AF.Exp)

        def rsqrt_to(dst, src_ps, inv_n, tmp):
            P.act(tmp, src_ps, AF.Ln, bias=EPS, scale=inv_n)
            P.act(dst, tmp, AF.Exp, scale=-0.5)

        def norm_tile(gcol, t, bk=7):
            tok = slice(t * TT, (t + 1) * TT)
            ssp = bank(bk)
            for q4 in range(4):
                P.act(nsq[:], X[:, 2 * q4:2 * q4 + 2, tok], AF.Square)

                def fn(eng, q4=q4, ssp=ssp):
                    ins = None
                    for k in range(2):
                        ins = eng.matmul(ssp, ones[:], nsq[:, k, :], start=(q4 == 0 and k == 0), stop=(q4 == 3 and k == 1))
                    return ins
                P.op("pe", fn, [ones[:], nsq[:]], [ssp])
            rsqrt_to(nrstd[:], ssp, 1.0 / D, nt1[:])
            for k in range(8):
                P.stt(H[:, k, tok], X[:, k, tok], sm[:, gcol + k:gcol + k + 1], nrstd[:], ALU.mult, ALU.mult)

        def norm_phase(gcol):
            for t in range(NT):
                norm_tile(gcol, t, 6 + t % 2)

        def out_proj_add(wo, nk, y, tok, banks=(0, 1, 2, 3, 4, 5, 6, 7)):
            for oc in range(8):
                pb = bank(banks[oc % len(banks)])
                P.mm(pb, [(wo[:, k, oc * 128:(oc + 1) * 128], y[:, k, :]) for k in range(nk)])
                P.tt("dve", X[:, oc, tok], X[:, oc, tok], pb, ALU.add)

        def proj_fm(pb, wv, c0, m, tok):
            P.mm(pb[0:m, :] if m < 128 else pb, [(wv[:, k, c0:c0 + m], H[:, k, tok]) for k in range(8)])

        def lane(*gens):
            for g_ in gens:
                yield from g_

        def rr(*gens):
            gens = list(gens)
            while gens:
                for g_ in list(gens):
                    try:
                        next(g_)
                    except StopIteration:
                        gens.remove(g_)

        def stage_gla_all():
            wviews = {}
            base = wst["cur"]

            def getw(c):
                if c not in wviews:
                    assert wst["cur"] == base + c
                    (v,) = wacquire(1)
                    wviews[c] = (v[:, 0:6272].rearrange("p (k n) -> p k n", k=8),
                                 v[:, 6272:8320].rearrange("p (k n) -> p k n", k=2),
                                 v[0:16, 8320:8448])
                return wviews[c]
            SF.reset(); SB.reset()
            sp_ = SF.take(512); nb = SF.take(512); ea = SF.take(512); eb_ = SF.take(512); tmp = SF.take(512)
            t1 = SF.take(512); orstd = SF.take(512); yt = SF.take(512)
            S = SF.take(128)
            decs = [SF.take(128) for _ in range(2)]
            bufs = []
            for _ in range(2):
                bufs.append(dict(q_in=SB.take(512), k_in=SB.take(512),
                                 k_ot=SB.take(512).rearrange("p (b n) -> p b n", b=4),
                                 v_tok=SB.take(1024).rearrange("p (b n) -> p b n", b=4),
                                 silg=SB.take(1024).rearrange("p (h n) -> p h n", h=2)))
            k_oT = SB.take(512)
            glow = k_oT
            osq_flat = SB.take(1024)
            osq = osq_flat.rearrange("p (h n) -> p h n", h=2)
            sT_all = osq_flat.rearrange("p (h b n) -> p h b n", h=2, b=4)
            ygla = SB.take(1024).rearrange("p (h n) -> p h n", h=2)
            S_bfa = SB.take(512).rearrange("p (b n) -> p b n", b=4)
            nb3 = nb.rearrange("p (b n) -> p b n", b=4)
            tmp3 = tmp.rearrange("p (b n) -> p b n", b=4)

            def prep(c, t):
                win, wo, wg2 = getw(c)
                tok = slice(t * TT, (t + 1) * TT)
                B_ = bufs[t % 2]
                q_in, k_in, k_ot, v_tok, silg = B_["q_in"], B_["k_in"], B_["k_ot"], B_["v_tok"], B_["silg"]
                dec = decs[t % 2]
                glp = bank(3)
                proj_fm(glp, win, 512, 16, tok)
                yield
                P.copy("act", glow[0:16, :], glp[0:16, :])
                yield
                zp = bank(4)
                P.mm(zp, [(wg2, glow[0:16, :])])
                yield
                P.act(ea, zp, AF.Exp, bias=negb[:, c:c + 1], scale=-1.0)
                P.act(sp_, ea, AF.Ln, bias=1.0)
                yield
                P.scan(nb, msk[:], sp_, 0.0, ALU.mult, ALU.add)
                qp = bank(3); proj_fm(qp, win, 0, 128, tok)
                kp = bank(4); proj_fm(kp, win, 128, 128, tok)
                yield
                P.act(dec[:, 0:4], nb3[:, :, 127], AF.Exp, scale=-1.0 / 16)
                P.act(ea, nb, AF.Exp, scale=-1.0 / 16)
                P.act(eb_, nb, AF.Exp, scale=1.0 / 16)
                yield
                P.stt(q_in, qp, 0.125, ea, ALU.mult, ALU.mult)
                P.tt("dve", k_in, kp, eb_, ALU.mult)
                P.tt("dve", tmp3, nb3, nb3[:, :, 127:128].to_broadcast([128, 4, 128]), ALU.subtract)
                yield
                P.act(ea, tmp, AF.Exp, scale=1.0 / 16)
                yield
                P.tt("dve", k_oT, kp, ea, ALU.mult)
                yield
                tpb = bank(5)
                tp3 = tpb.rearrange("p (b n) -> p b n", b=4)
                P.mm_multi([(tp3[:, b, :], [(k_oT[:, b * 128:(b + 1) * 128], ident)]) for b in range(4)])
                yield
                P.copy("act", k_ot, tp3)
                yield
                for b in range(4):
                    vb = bank(3 if b % 2 else 5)
                    P.mm(vb[:, 0:256], [(H[:, k, t * TT + b * 128:t * TT + (b + 1) * 128], win[:, k, 256:512]) for k in range(8)])
                    yield
                    P.copy("act", v_tok[:, b, :], vb[:, 0:256])
                    yield
                for hl in range(2):
                    gp_ = bank(3 + hl)
                    proj_fm(gp_, win, 528 + 128 * hl, 128, tok)
                    yield
                    P.act(silg[:, hl, :], gp_, AF.Silu)
                    yield

            def blocks(c, t):
                win, wo, wg2 = getw(c)
                if t == 0:
                    P.memset("dve", S, 0.0)
                tok = slice(t * TT, (t + 1) * TT)
                B_ = bufs[t % 2]
                q_in, k_in, k_ot, v_tok, silg = B_["q_in"], B_["k_in"], B_["k_ot"], B_["v_tok"], B_["silg"]
                dec = decs[t % 2]
                op2 = bank2(6)
                kv4 = ps[:, 0:2, :].rearrange("p a (b n) -> p (a b) n", b=2)
                P.mm_multi([(kv4[:, b, :], [(k_ot[:, b, :], v_tok[:, b, :])]) for b in range(4)])
                yield
                for b in range(4):
                    P.copy("dve", S_bfa[:, b, :], S)
                    for hl in range(2):
                        pr = slice(64 * hl, 64 * hl + 64)
                        P.stt(S[pr, :], S[pr, :], dec[pr, b:b + 1], kv4[pr, b, hl * 128:(hl + 1) * 128], ALU.mult, ALU.add)
                yield
                sc4 = ps[:, 0:2, :].rearrange("p h (b n) -> p h b n", b=4)
                P.mm_multi([(sc4[:, hl, b, :], [(k_in[64 * hl:64 * hl + 64, b * 128:(b + 1) * 128],
                                                 q_in[64 * hl:64 * hl + 64, b * 128:(b + 1) * 128])])
                            for b in range(4) for hl in range(2)])
                yield
                P.tt("dve", sT_all, sc4, tri.unsqueeze(1).unsqueeze(1).to_broadcast([128, 2, 4, 128]), ALU.mult)
                yield
                P.mm_multi([(op2[:, hl, b * 128:(b + 1) * 128],
                             [(v_tok[:, b, hl * 128:(hl + 1) * 128], sT_all[:, hl, b, :]),
                              (S_bfa[64 * hl:64 * hl + 64, b, :], q_in[64 * hl:64 * hl + 64, b * 128:(b + 1) * 128])])
                            for b in range(4) for hl in range(2)])
                yield
                P.act(osq, op2, AF.Square)
                yield
                for hl in range(2):
                    ssp = bank(hl)
                    P.mm(ssp, [(ones[:], osq[:, hl, :])])
                    yield
                    rsqrt_to(orstd, ssp, 1.0 / 128, t1)
                    yield
                    P.stt(yt, op2[:, hl, :], sm[:, SM_OGAIN:SM_OGAIN + 1], orstd, ALU.mult, ALU.mult)
                    P.tt("dve", ygla[:, hl, :], yt, silg[:, hl, :], ALU.mult)
                    yield
                obanks = (0, 1, 2, 6, 7)
                for oc in range(8):
                    pb = bank(obanks[oc % 5])
                    P.mm(pb, [(wo[:, k, oc * 128:(oc + 1) * 128], ygla[:, k, :]) for k in range(2)])
                    yield
                    P.tt("dve", X[:, oc, tok], X[:, oc, tok], pb, ALU.add)
                    yield

            rr(prep(0, 0))
            for c in range(2):
                for t in range(NT):
                    nxt = (c, t + 1) if t + 1 < NT else ((c + 1, 0) if c + 1 < 2 else None)
                    if nxt is not None:
                        rr(blocks(c, t), prep(*nxt))
                    else:
                        rr(blocks(c, t))
                wload_upto(base + c + 1 + NSLOT)

        def stage_conv(c, post=None):
            (v,) = wacquire(1)
            win = v[:, 0:6144].rearrange("p (k n) -> p k n", k=8)
            wo = v[:, 6144:8192].rearrange("p (k n) -> p k n", k=2)
            SF.reset(); SB.reset()
            ccs = SF.take(1024).rearrange("p (j n) -> p j n", j=2)
            U = SF.take(2 * 516).rearrange("p (j n) -> p j n", j=2)
            acc = SF.take(1024).rearrange("p (j n) -> p j n", j=2)
            ycs = [SB.take(1024).rearrange("p (j n) -> p j n", j=2) for _ in range(2)]
            P.memset("dve", U[:, :, 0:2], 0.0)

            def chain(t, j):
                tok = slice(t * TT, (t + 1) * TT)
                cb = bank(j); cc = bank(2 + j); ci = bank(4 + j)
                proj_fm(cc, win, 256 + 128 * j, 128, tok)
                proj_fm(ci, win, 512 + 128 * j, 128, tok)
                yield
                P.copy("act", ccs[:, j, :], cc)
                proj_fm(cb, win, 128 * j, 128, tok)
                yield
                P.tt("dve", U[:, j, 2:514], ci, ccs[:, j, :], ALU.mult)
                yield
                cw = SM_CONVW + 3 * (2 * c + j)
                P.ts("dve", acc[:, j, :], U[:, j, 2:514], sm[:, cw + 2:cw + 3], ALU.mult)
                P.stt(acc[:, j, :], U[:, j, 1:513], sm[:, cw + 1:cw + 2], acc[:, j, :], ALU.mult, ALU.add)
                P.stt(acc[:, j, :], U[:, j, 0:512], sm[:, cw:cw + 1], acc[:, j, :], ALU.mult, ALU.add)
                yield
                P.tt("dve", ycs[t % 2][:, j, :], cb, acc[:, j, :], ALU.mult)
                P.copy("dve", U[:, j, 0:2], U[:, j, 512:514])
                yield

            def outl(t):
                tok = slice(t * TT, (t + 1) * TT)
                for oc in range(8):
                    pb = bank(6 + oc % 2)
                    P.mm(pb, [(wo[:, k, oc * 128:(oc + 1) * 128], ycs[t % 2][:, k, :]) for k in range(2)])
                    yield
                    P.tt("dve", X[:, oc, tok], X[:, oc, tok], pb, ALU.add)
                    yield
                if post is not None:
                    norm_tile(post, t, 7)
                    yield

            rr(chain(0, 0), chain(0, 1))
            for t in range(NT):
                if t + 1 < NT:
                    rr(chain(t + 1, 0), chain(t + 1, 1), outl(t))
                else:
                    rr(outl(t))
            wrelease()

        def stage_xa(l, s, post=None):
            SF.reset(); SB.reset()
            kT = SB.take(2048).rearrange("p (c n) -> p c n", c=8)
            vtk = SB.take(2048).rearrange("p (b n) -> p b n", b=2)
            sb_mark = SB.o
            mT = SF.take(2048).rearrange("p (k n) -> p k n", k=8)
            t1 = SF.take(512); rstd = SF.take(512); rden = SF.take(512)
            sqm = SB.take(2048).rearrange("p (k n) -> p k n", k=8)
            Hm = SB.take(2048).rearrange("p (k n) -> p k n", k=8)
            P.dma("sp", mT.rearrange("p k n -> p (k n)"), memT[s], "ldm")
            P.act(sqm, mT, AF.Square)
            ssp = bank(7)
            P.mm(ssp[:, 0:256], [(ones[:], sqm[:, k, :]) for k in range(8)])
            rsqrt_to(rstd[:, 0:256], ssp[:, 0:256], 1.0 / D, t1[:, 0:256])
            for k in range(8):
                P.stt(Hm[:, k, :], mT[:, k, :], sm[:, SM_MEMN + 8 * l + k:SM_MEMN + 8 * l + k + 1], rstd[:, 0:256], ALU.mult, ALU.mult)
            (v,) = wacquire(1)
            wk = v[:, 0:8192].rearrange("p (k n) -> p k n", k=8)
            sqk = sqm[:, 0:2, :]
            for h in range(4):
                kp = bank(h % 2)
                kp2 = kp.rearrange("p (c n) -> p c n", c=2)
                for c2 in range(2):
                    oc = 2 * h + c2
                    P.mm(kp2[:, c2, :], [(wk[:, k, oc * 128:(oc + 1) * 128], Hm[:, k, :]) for k in range(8)])
                P.act(sqk, kp2, AF.Square)
                ssp = bank(2 + h % 2)
                P.mm(ssp[:, 0:256], [(ones[:], sqk[:, c2, :]) for c2 in range(2)])
                rsqrt_to(rstd[:, 0:256], ssp[:, 0:256], 1.0 / 256, t1[:, 0:256])
                for c2 in range(2):
                    P.stt(kT[:, 2 * h + c2, :], kp2[:, c2, :], sm[:, SM_XKG + 2 * l + c2:SM_XKG + 2 * l + c2 + 1], rstd[:, 0:256], ALU.mult, ALU.mult)
            wrelease()
            (v,) = wacquire(1)
            wv = v[:, 0:8192].rearrange("p (k n) -> p k n", k=8)
            for kb in range(2):
                for hf in range(2):
                    vb = bank(4 + 2 * kb + hf)
                    P.mm(vb, [(Hm[:, k, kb * 128:(kb + 1) * 128], wv[:, k, hf * 512:(hf + 1) * 512]) for k in range(8)])
                    P.copy("act", vtk[:, kb, hf * 512:(hf + 1) * 512], vb)
            wrelease()
            SB.o = sb_mark
            SF.reset()
            sqps = [SB.take(1024).rearrange("p (c n) -> p c n", c=2) for _ in range(2)]
            qns = [SB.take(1024).rearrange("p (c n) -> p c n", c=2) for _ in range(2)]
            oT = SB.take(2048).rearrange("p (c n) -> p c n", c=4)
            t1s = [SF.take(512) for _ in range(2)]
            rstds = [SF.take(512) for _ in range(2)]
            rdens = [SF.take(512) for _ in range(2)]
            for j in range(2):
                (v,) = wacquire(1)
                wq = v[:, 0:4096].rearrange("p (k n) -> p k n", k=8)
                wo = v[:, 4096:8192].rearrange("p (k n) -> p k n", k=4)
                prev_tok = None
                for t in range(NT):
                    tok = slice(t * TT, (t + 1) * TT)
                    qps, ssps = [], []
                    for hl in range(2):
                        qp = bank2(4 * hl)
                        for c2 in range(2):
                            proj_fm(qp[:, c2, :], wq, hl * 256 + c2 * 128, 128, tok)
                        P.act(sqps[hl], qp, AF.Square)
                        ssp = bank(4 * hl + 2)
                        P.mm(ssp, [(ones[:], sqps[hl][:, c2, :]) for c2 in range(2)])
                        qps.append(qp); ssps.append(ssp)
                    if prev_tok is not None:
                        out_proj_add(wo, 4, oT, prev_tok, banks=(3, 7))
                        if post is not None and j == 1:
                            norm_tile(post, t - 1, 3)
                    for hl in range(2):
                        rsqrt_to(rstds[hl], ssps[hl], 1.0 / 256, t1s[hl])
                        for c2 in range(2):
                            P.stt(qns[hl][:, c2, :], qps[hl][:, c2, :], sm[:, SM_XQG + 2 * l + c2:SM_XQG + 2 * l + c2 + 1], rstds[hl], ALU.mult, ALU.mult)
                    for hl in range(2):
                        h = 2 * j + hl
                        sp2 = bank2(4 * hl + 2)
                        for kb in range(2):
                            P.mm(sp2[:, kb, :], [(kT[:, 2 * h + c2, kb * 128:(kb + 1) * 128], qns[hl][:, c2, :]) for c2 in range(2)])
                        P.act(sqps[hl], sp2, AF.Exp, scale=1.0 / 16)
                    for hl in range(2):
                        h = 2 * j + hl
                        dp = bank(4 * hl + 2)
                        P.mm(dp, [(ones[:], sqps[hl][:, kb, :]) for kb in range(2)])
                        P.act(t1s[hl], dp, AF.Ln)
                        P.act(rdens[hl], t1s[hl], AF.Exp, scale=-1.0)
                        op2 = bank2(4 * hl)
                        for c2 in range(2):
                            P.mm(op2[:, c2, :], [(vtk[:, kb, h * 256 + c2 * 128:h * 256 + (c2 + 1) * 128], sqps[hl][:, kb, :]) for kb in range(2)])
                        for c2 in range(2):
                            P.tt("dve", oT[:, hl * 2 + c2, :], op2[:, c2, :], rdens[hl], ALU.mult)
                    prev_tok = tok
                out_proj_add(wo, 4, oT, prev_tok)
                if post is not None and j == 1:
                    norm_tile(post, NT - 1, 7)
                wrelease()

        def stage_ffn(l, post=None, defer_release=False):
            SF.reset(); SB.reset()
            sgs = [SF.take(512) for _ in range(2)]
            a = SB.take(2048).rearrange("p (c n) -> p c n", c=4)
            for (c0, n) in FFN_GROUPS:
                va, vb_ = wacquire(2)
                wgu = va[:, 0:8 * 256 * n].rearrange("p (k n) -> p k n", k=8)
                wd = vb_[:, 0:1024 * n].rearrange("p (k n) -> p k n", k=n)
                for t in range(NT):
                    tok = slice(t * TT, (t + 1) * TT)
                    for ch in range(n):
                        gp = bank(2 * (ch % 2)); proj_fm(gp, wgu, ch * 128, 128, tok)
                        up = bank(2 * (ch % 2) + 1); proj_fm(up, wgu, n * 128 + ch * 128, 128, tok)
                        sg = sgs[ch % 2]
                        P.act(sg, gp, AF.Silu)
                        P.tt("dve", a[:, ch, :], up, sg, ALU.mult)
                    out_proj_add(wd, n, a, tok, banks=(4, 5, 6, 7))
                    if post is not None and c0 == FFN_GROUPS[-1][0]:
                        norm_tile(post, t, 7)
                if not (defer_release and c0 == FFN_GROUPS[-1][0]):
                    wrelease()

        def stage_swa_all(s, post=None):
            SF.reset(); SB.reset()
            Ct = SF.take(512); St = SF.take(512)
            f = [SF.take(512) for _ in range(6)]
            pif = SF.take(512)
            pi_ = posi[:]
            qrots = [SB.take(1024).rearrange("p (c n) -> p c n", c=2) for _ in range(2)]
            krot = SB.take(T)
            vvt = SB.take(NB * 128).rearrange("p (b n) -> p b n", b=NB)
            sq = SB.take(512); qn = SB.take(512)
            pTs = [SB.take(1024).rearrange("p (h w c n) -> p h w c n", h=2, w=2, c=2) for _ in range(2)]
            yT_ = SB.take(1024).rearrange("p (c n) -> p c n", c=2)
            esrs = [SB.take(512) for _ in range(2)]
            wviews = {}
            base = wst["cur"]

            def getw(g):
                if g not in wviews:
                    assert wst["cur"] == base + g
                    (v,) = wacquire(1)
                    wviews[g] = (v[:, 0:4096].rearrange("p (k n) -> p k n", k=8),
                                 v[:, 4096:6144].rearrange("p (k n) -> p k n", k=2))
                    esr = esrs[g % 2]
                    for hf in range(2):
                        for c_ in range(2):
                            o_ = (2 * hf + c_) * 128
                            P.copy("dve", esr[0:1, o_:o_ + 128], esink[0:1, 4 * g + 2 * c_ + hf:4 * g + 2 * c_ + hf + 1].to_broadcast([1, 128]))
                return wviews[g]

            def tables(t):
                tok = slice(t * TT, (t + 1) * TT)
                A, B = f[4], f[5]
                P.dma("sp", pi_, posd[s, :, tok].partition_broadcast(128), "ldp")
                P.copy("dve", A, pi_)
                P.ts("dve", A, A, sm[:, SM_INVF:SM_INVF + 1], ALU.mult)
                yield
                for which, dst in ((0, St), (1, Ct)):
                    if which == 1:
                        P.ts("dve", A, A, PI / 2, ALU.add)
                    P.ts("dve", B, A, 1.0 / TWO_PI, ALU.mult)
                    P.copy("dve", pi_, B)
                    P.copy("dve", B, pi_)
                    yield
                    P.stt(B, B, -TWO_PI, A, ALU.mult, ALU.add)
                    P.ts("dve", pif, B, PI, ALU.is_gt, s2=TWO_PI, op1=ALU.mult)
                    P.tt("dve", B, B, pif, ALU.subtract)
                    yield
                    P.act(dst, B, AF.Sin)
                    yield

            def chunk(g, t, ci):
                win, wo = getw(g)
                tok = slice(t * TT, (t + 1) * TT)
                pb = bank(4)
                proj_fm(pb, win, 128 * ci, 128, tok)
                yield
                P.act(sq, pb, AF.Square)
                yield
                ssp = bank(5)
                P.mm(ssp, [(bd[:], sq)])
                yield
                P.act(pif, ssp, AF.Ln, bias=EPS, scale=1.0 / 64)
                P.act(f[4], pif, AF.Exp, scale=-0.5)
                yield
                gcol = SM_SQG if ci < 2 else SM_SKG
                P.stt(qn, pb, sm[:, gcol:gcol + 1], f[4], ALU.mult, ALU.mult)
                yield
                pq = bank(6)
                P.mm(pq, [(pmT, qn)])
                P.tt("dve", f[4], qn, Ct, ALU.mult)
                yield
                P.tt("dve", f[5], pq, St, ALU.mult)
                dst = qrots[t % 2][:, ci, :] if ci < 2 else krot[:, tok]
                P.tt("dve", dst, f[4], f[5], ALU.add)
                yield

            def vproj(g, t):
                win, wo = getw(g)
                vb = bank(7)
                vb3 = vb.rearrange("p (b n) -> p b n", b=4)
                P.mm_multi([(vb3[:, b, :], [(H[:, k, t * TT + b * 128:t * TT + (b + 1) * 128], win[:, k, 384:512]) for k in range(8)]) for b in range(4)])
                yield
                P.copy("act", vvt[:, 4 * t:4 * t + 4, :], vb3)
                yield

            def block(g, t, b):
                esr = esrs[g % 2]
                n = 4 * t + b
                bt = slice(b * 128, (b + 1) * 128)
                p_ = b % 2
                nw = 2 if n > 0 else 1
                qrot = qrots[t % 2]
                pT = pTs[p_]
                fa, fb = f[2 * p_], f[2 * p_ + 1]
                sc = [bank(2 * p_ + hf).rearrange("p (w c n) -> p w c n", w=2, c=2) for hf in range(2)]
                items = []
                for w_ in range(nw):
                    kb_ = n - w_
                    for c_ in range(2):
                        for hf in range(2):
                            pr = slice(64 * hf, 64 * hf + 64)
                            items.append((sc[hf][:, w_, c_, :], (ident, cm[:, 4 + w_, :]),
                                          (krot[pr, kb_ * 128:(kb_ + 1) * 128], qrot[pr, c_, bt])))

                def fn(eng, items=items):
                    ins = None
                    ni = len(items)
                    for i, (o_, (ml, mr), _) in enumerate(items):
                        ins = eng.matmul(o_, ml, mr, start=(i < 2), stop=False)
                    for i, (o_, _, (sl, sr)) in enumerate(items):
                        ins = eng.matmul(o_, sl, sr, start=False, stop=(i >= ni - 2))
                    return ins
                rd_ = []
                for o_, (ml, mr), (sl, sr) in items:
                    rd_ += [ml, mr, sl, sr]
                P.op("pe", fn, rd_, [it[0] for it in items])
                yield
                for hf in range(2):
                    P.act(pT[:, hf, 0:nw], sc[hf][:, 0:nw], AF.Exp, scale=0.125)
                yield
                db = bank(2 * p_); ob = bank(2 * p_ + 1)
                ob4 = ob.rearrange("p (h n) -> p h n", h=4)
                grp = []
                for hl in range(4):
                    c_, hf = hl // 2, hl % 2
                    grp.append((ob4[:, hl, :], [(vvt[:, n - w_, :], pT[:, hf, w_, c_, :]) for w_ in range(nw)]))
                P.mm_multi(grp)
                db3 = db.rearrange("p (h m) -> p h m", h=2)
                P.mm_multi([(db3[:, hf, :], [(ones[:], pT[:, hf, w_].rearrange("p c n -> p (c n)")) for w_ in range(nw)]
                             + [(ones[0:1, :], esr[0:1, hf * 256:(hf + 1) * 256])]) for hf in range(2)])
                yield
                P.act(fa, db, AF.Ln)
                P.act(fb, fa, AF.Exp, scale=-1.0)
                yield
                rd = fb.rearrange("p (h c n) -> p h c n", h=2, c=2)
                ob5 = ob.rearrange("p (c h n) -> p c h n", c=2, h=2)
                for hf in range(2):
                    pr = slice(64 * hf, 64 * hf + 64)
                    P.tt("dve", yT_[pr, :, bt], ob5[pr, :, hf, :], rd[pr, hf, :, :], ALU.mult)
                yield

            def outl(g, t):
                win, wo = getw(g)
                tok = slice(t * TT, (t + 1) * TT)
                obanks = (4, 5, 6, 7) if (g == 3 and t == NT - 1) else (0, 1, 2, 3)
                for oc in range(8):
                    pb = bank(obanks[oc % 4])
                    P.mm(pb, [(wo[:, k, oc * 128:(oc + 1) * 128], yT_[:, k, :]) for k in range(2)])
                    yield
                    P.tt("dve", X[:, oc, tok], X[:, oc, tok], pb, ALU.add)
                    yield
                if post is not None and g == 3:
                    norm_tile(post, t, 7 if t == NT - 1 else 3)
                    yield

            def prep(g, t):
                return lane(tables(t), chunk(g, t, 0), chunk(g, t, 1), chunk(g, t, 2), vproj(g, t))

            rr(prep(0, 0))
            for g in range(4):
                for t in range(NT):
                    la = lane(block(g, t, 0), block(g, t, 1), block(g, t, 2), block(g, t, 3), outl(g, t))
                    nxt = (g, t + 1) if t + 1 < NT else ((g + 1, 0) if g + 1 < 4 else None)
                    if nxt is not None:
                        rr(la, prep(*nxt))
                    else:
                        rr(la)
                wload_upto(base + g + 1 + NSLOT)

        di = {"i": 0}

        def dump():
            if dbg and di["i"] < NDBG:
                for k in range(8):
                    P.dma("sp", dbgd[di["i"], :, k * T:(k + 1) * T], X[:, k, :], "dbg", group_final=8 - k)
                di["i"] += 1

        stc = {"i": 0}

        def run(fn, *a):
            stc["i"] += 1
            if stc["i"] <= maxstage:
                fn(*a)

        def xload(s_, t):
            tok = slice(t * TT, (t + 1) * TT)
            P.dma("sp", X[:, :, tok], xT[s_].rearrange("p (k n) -> p k n", k=8)[:, :, tok], "lx%d" % (t % 8))

        for s in range(NSEQ):
            if s == 0:
                xload(s, 0)
                wload_upto(1)
                for t in range(1, NT):
                    xload(s, t)
                wload_upto(NSLOT)
            for li, l in enumerate(layers):
                if li == 0:
                    run(norm_phase, SM_MIX + 8 * l)
                if l == 0:
                    run(stage_gla_all); run(stage_conv, 0); run(stage_conv, 1, SM_XAN + 8 * l)
                else:
                    run(stage_swa_all, s, SM_XAN + 8 * l)
                if s == 0:
                    dump()
                run(stage_xa, l, s, SM_FFN + 8 * l)
                if s == 0:
                    dump()
                nxt = layers[li + 1] if li + 1 < len(layers) else None
                run(stage_ffn, l, (SM_MIX + 8 * nxt) if nxt is not None else None, nxt is None)
                if s == 0:
                    dump()
            for t in range(NT):
                tok = slice(t * TT, (t + 1) * TT)
                P.dma("sp", yT[s].rearrange("p (k n) -> p k n", k=8)[:, :, tok], X[:, :, tok], "sx%d" % (t % 8))
                if s + 1 < NSEQ:
                    xload(s + 1, t)
            wrelease()
        P.wait_all_dma("sp", ["sx%d" % i for i in range(8)] + ["dbg"])
        P.emit()
        print("ops", P.nops, "waits", P.nwaits, {e: len(q) for e, q in P.q.items()})
    return nc


def _piece_meta():
    sizes, lay = [], []
    for l in range(2):
        if l == 0:
            sizes += [8448, 8448, 8192, 8192]; lay += [0] * 4
        else:
            sizes += [6144] * 4; lay += [1] * 4
        sizes += [8192, 8192, 8192, 8192]; lay += [l] * 4
        for (c0, n) in FFN_GROUPS:
            sizes += [8 * 256 * n, 1024 * n]; lay += [l, l]
    return sizes, lay


PIECE_SIZES, PIECE_LAYER = _piece_meta()


def prep_inputs(inp, T, nseq_per_core, ncores, seq_ids=None, layers=(0, 1)):
    pieces = build_pieces(inp)
    assert [p[1].shape[1] for p in pieces] == PIECE_SIZES, [p[1].shape[1] for p in pieces]
    pieces = [pieces[i] for i in range(len(pieces)) if PIECE_LAYER[i] in layers]
    wbig = np.ascontiguousarray(np.concatenate([p[1] for p in pieces], axis=1).astype(np.float32))
    small = build_small(inp)
    cmat = build_cmat()
    x = np.asarray(inp["x"]); mem = np.asarray(inp["mem"]); pos = np.asarray(inp["positions"])
    maps = []
    for c in range(ncores):
        ids = seq_ids[c] if seq_ids is not None else list(range(c * nseq_per_core, (c + 1) * nseq_per_core))
        xs = x[ids][:, :T, :]
        xTt = xs.transpose(0, 2, 1).reshape(len(ids), 8, 128, T).transpose(0, 2, 1, 3).reshape(len(ids), 128, 8 * T)
        ms = mem[ids]
        mTt = ms.transpose(0, 2, 1).reshape(len(ids), 8, 128, MEM).transpose(0, 2, 1, 3).reshape(len(ids), 128, 8 * MEM)
        maps.append({"xT": np.ascontiguousarray(xTt, dtype=np.float32), "memT": np.ascontiguousarray(mTt, dtype=np.float32),
                     "pos": np.ascontiguousarray(pos[ids][:, None, :T].astype(np.int32)),
                     "wbig": wbig, "small": small, "cmat": cmat})
    return maps


def unpack_out(yT, T):
    n = yT.shape[0]
    return yT.reshape(n, 128, 8, T).transpose(0, 2, 1, 3).reshape(n, 1024, T).transpose(0, 2, 1)


def kernel(**inputs):
    inp = {k: np.asarray(v) for k, v in inputs.items()}
    T = 2048
    nc = build_program(T, 2)
    maps = prep_inputs(inp, T, 2, NCORES)
    res = run_bass_kernel_spmd(nc, maps, core_ids=list(range(NCORES)))
    outs = [unpack_out(r["yT"], T) for r in res.results]
    return np.ascontiguousarray(np.concatenate(outs, axis=0).astype(np.float32))
```

```python
import concourse.bass as bass
import concourse.mybir as mybir

F32 = mybir.dt.float32
BF16 = mybir.dt.bfloat16
I32 = mybir.dt.int32
ALU = mybir.AluOpType
AF = mybir.ActivationFunctionType
AX = mybir.AxisListType

CELL = 512


def _dsize(dt):
    s = str(dt)
    if "64" in s:
        return 8
    if "32" in s:
        return 4
    if "16" in s:
        return 2
    return 1


def ap_cells(ap):
    sp = str(ap.space)
    if "SB" not in sp and "PSUM" not in sp:
        return None
    ds = _dsize(ap.dtype)
    pat = ap.ap
    row = pat[0][0]
    off = ap.offset % row if row > 0 else ap.offset
    ext = 1
    for st, cn in pat[1:]:
        ext += abs(st) * (cn - 1)
    b0 = off * ds
    b1 = (off + ext) * ds
    name = ap.name
    cell = 2048 if "PSUM" in sp else CELL
    return [(name, c) for c in range(b0 // cell, (b1 - 1) // cell + 1)]


class Prog:
    ENGS = ("pe", "act", "dve", "pool", "sp")

    def __init__(self, nc, same_engine_sync=True):
        self.nc = nc
        self.q = {e: [] for e in self.ENGS}
        self.cnt = {e: 0 for e in self.ENGS}
        self.sems = {}
        self.seen = {e: {} for e in self.ENGS}
        self.cells = {}
        self.dma_cnt = {}
        self.same_engine_sync = same_engine_sync
        self.nwaits = 0
        self.nops = 0

    def add_sem(self, key, handle):
        self.sems[key] = handle

    def _deps(self, engine, reads, writes):
        deps = {}

        def add(tok):
            if tok is None:
                return
            k, v = tok
            if k == engine and (engine == "pe" or not self.same_engine_sync):
                return
            if deps.get(k, 0) < v:
                deps[k] = v

        rc, wc = [], []
        for ap in reads:
            c = ap_cells(ap)
            if c:
                rc += c
        for ap in writes:
            c = ap_cells(ap)
            if c:
                wc += c
        for c in rc:
            st = self.cells.get(c)
            if st:
                add(st[0])
        for c in wc:
            st = self.cells.get(c)
            if st:
                add(st[0])
                for tok in st[1].items():
                    add(tok)
        seen = self.seen[engine]
        waits = []
        for k, v in deps.items():
            if seen.get(k, 0) < v:
                seen[k] = v
                waits.append((k, v))
        return waits, rc, wc

    def _commit(self, tok, rc, wc):
        for c in rc:
            st = self.cells.get(c)
            if st is None:
                st = [None, {}]
                self.cells[c] = st
            k, v = tok
            if st[1].get(k, 0) < v:
                st[1][k] = v
        for c in wc:
            self.cells[c] = [tok, {}]

    def op(self, engine, fn, reads, writes):
        waits, rc, wc = self._deps(engine, reads, writes)
        self.cnt[engine] += 1
        tok = (engine, self.cnt[engine])
        self._commit(tok, rc, wc)
        self.q[engine].append((waits, fn, engine, 1))
        self.nwaits += len(waits)
        self.nops += 1

    def dma(self, queue, out, in_, semkey, group_final=None, **kw):
        waits, rc, wc = self._deps(queue, [in_], [out])
        self.dma_cnt[semkey] = self.dma_cnt.get(semkey, 0) + 16
        tok = (semkey, self.dma_cnt[semkey] + 16 * ((group_final or 1) - 1))
        self._commit(tok, rc, wc)

        def fn(eng, out=out, in_=in_, kw=kw):
            return eng.dma_start(out=out, in_=in_, **kw)

        self.q[queue].append((waits, fn, semkey, 16))
        self.nwaits += len(waits)

    def wait_all_dma(self, engine, keys):
        waits = []
        for k in keys:
            v = self.dma_cnt.get(k, 0)
            if v and self.seen[engine].get(k, 0) < v:
                self.seen[engine][k] = v
                waits.append((k, v))
        self.q[engine].append((waits, None, None, 0))

    def barrier(self, engines=("pe", "act", "dve")):
        for e in engines:
            waits = []
            for f in engines:
                if f == e:
                    continue
                v = self.cnt[f]
                if v and self.seen[e].get(f, 0) < v:
                    self.seen[e][f] = v
                    waits.append((f, v))
            if waits:
                self.q[e].append((waits, None, None, 0))

    def act(self, out, in_, func, bias=0.0, scale=1.0, accum_out=None):
        reads = [in_]
        if not isinstance(bias, (int, float)):
            reads.append(bias)
        if not isinstance(scale, (int, float)):
            reads.append(scale)
        writes = [out] + ([accum_out] if accum_out is not None else [])

        def fn(eng):
            kw = {}
            if accum_out is not None:
                kw["accum_out"] = accum_out
            return eng.activation(out=out, in_=in_, func=func, bias=bias, scale=scale, **kw)

        self.op("act", fn, reads, writes)

    def tt(self, eng_name, out, in0, in1, op):
        def fn(eng):
            return eng.tensor_tensor(out=out, in0=in0, in1=in1, op=op)

        self.op(eng_name, fn, [in0, in1], [out])

    def ts(self, eng_name, out, in0, s1, op0, s2=None, op1=None):
        reads = [in0]
        if not isinstance(s1, (int, float)):
            reads.append(s1)
        if s2 is not None and not isinstance(s2, (int, float)):
            reads.append(s2)

        def fn(eng):
            if op1 is None:
                return eng.tensor_single_scalar(out=out, in_=in0, scalar=s1, op=op0)
            return eng.tensor_scalar(out=out, in0=in0, scalar1=s1, scalar2=s2, op0=op0, op1=op1)

        self.op(eng_name, fn, reads, [out])

    def stt(self, out, in0, scalar, in1, op0, op1, eng_name="dve"):
        reads = [in0, in1]
        if not isinstance(scalar, (int, float)):
            reads.append(scalar)

        def fn(eng):
            return eng.scalar_tensor_tensor(out=out, in0=in0, scalar=scalar, in1=in1, op0=op0, op1=op1)

        self.op(eng_name, fn, reads, [out])

    def copy(self, eng_name, out, in_):
        if eng_name == "act":
            def fn(eng):
                return eng.copy(out=out, in_=in_)
        else:
            def fn(eng):
                return eng.tensor_copy(out=out, in_=in_)
        self.op(eng_name, fn, [in_], [out])

    def memset(self, eng_name, out, val):
        def fn(eng):
            return eng.memset(out, val)

        self.op(eng_name, fn, [], [out])

    def scan(self, out, d0, d1, initial, op0, op1):
        reads = [d0, d1]
        if not isinstance(initial, (int, float)):
            reads.append(initial)

        def fn(eng):
            return eng.tensor_tensor_scan(out=out, data0=d0, data1=d1, initial=initial, op0=op0, op1=op1)

        self.op("dve", fn, reads, [out])

    def mm(self, out, pairs, extra_reads=()):
        reads = list(extra_reads)
        for l, r in pairs:
            reads += [l, r]
        n = len(pairs)

        def fn(eng):
            ins = None
            for i, (l, r) in enumerate(pairs):
                ins = eng.matmul(out, l, r, start=(i == 0), stop=(i == n - 1))
            return ins

        self.op("pe", fn, reads, [out])

    def mm_multi(self, groups):
        reads, writes = [], []
        for out, pairs in groups:
            writes.append(out)
            for l, r in pairs:
                reads += [l, r]

        def fn(eng):
            ins = None
            for out, pairs in groups:
                n = len(pairs)
                for i, (l, r) in enumerate(pairs):
                    ins = eng.matmul(out, l, r, start=(i == 0), stop=(i == n - 1))
            return ins

        self.op("pe", fn, reads, writes)

    def emit(self):
        nc = self.nc
        sems = self.sems
        q = self.q

        def run(eng, name):
            for waits, fn, sigkey, inc in q[name]:
                for k, v in waits:
                    eng.wait_ge(sems[k], v)
                if fn is None:
                    continue
                ins = fn(eng)
                if sigkey is not None:
                    ins.then_inc(sems[sigkey], inc)

        with nc.Block() as block:
            @block.tensor
            def _(eng):
                run(eng, "pe")

            @block.scalar
            def _(eng):
                run(eng, "act")

            @block.vector
            def _(eng):
                run(eng, "dve")

            @block.gpsimd
            def _(eng):
                run(eng, "pool")

            @block.sync
            def _(eng):
                run(eng, "sp")

import numpy as np
from contextlib import ExitStack
from concourse.bass_utils import run_bass_kernel_spmd

NCORES = 8
SAME_ENGINE_SYNC = True
FLAGS = ''
D = 1024
KC = 8
TT = 512
FH = 2816
SLOT = 8704
NSLOT = 3
EPS = 1e-6
MEM = 256
TWO_PI = 6.283185307179586
PI = 3.141592653589793

SM_MIX, SM_XAN, SM_FFN, SM_MEMN = 0, 16, 32, 48
SM_GBIAS, SM_OGAIN, SM_CONVW, SM_SQG, SM_SKG, SM_SINK = 64, 66, 67, 79, 80, 81
SM_XQG, SM_XKG, SM_INVF, NSM = 97, 101, 105, 106

FFN_GROUPS = [(0, 4), (4, 4), (8, 4), (12, 4), (16, 4), (20, 2)]


def _fm(w):
    k, n = w.shape
    return np.ascontiguousarray(w.reshape(k // 128, 128, n).transpose(1, 0, 2)).reshape(128, -1)


def build_pieces(inp):
    pieces = []
    for l in range(2):
        if l == 0:
            w_in = inp["hyb_w_in"][0]
            w_out = inp["hyb_w_out"][0]
            wg2 = inp["gla_w_gate2"][0]
            for c in range(2):
                cols = np.concatenate([w_in[:, 128 * c:128 * c + 128], w_in[:, 256 + 128 * c:256 + 128 * c + 128],
                                       w_in[:, 512 + 256 * c:512 + 256 * c + 256], w_in[:, 1024:1040],
                                       w_in[:, 1040 + 256 * c:1040 + 256 * c + 256]], axis=1)
                g2 = np.zeros((128, 128), np.float32)
                g2[0:16, :] = wg2[:, 128 * c:128 * c + 128]
                pieces.append(("gla%d" % c, np.concatenate([_fm(cols), _fm(w_out[256 * c:256 * c + 256, :]), g2], axis=1)))
            for c in range(2):
                cols = np.concatenate([w_in[:, 1552 + 256 * c:1552 + 256 * c + 256], w_in[:, 2064 + 256 * c:2064 + 256 * c + 256],
                                       w_in[:, 2576 + 256 * c:2576 + 256 * c + 256]], axis=1)
                pieces.append(("conv%d" % c, np.concatenate([_fm(cols), _fm(w_out[512 + 256 * c:512 + 256 * c + 256, :])], axis=1)))
        else:
            w_qkv = inp["swa_w_qkv"][0]
            w_out = inp["swa_w_out"][0]
            for g in range(4):
                kc = w_qkv[:, 1024 + 64 * g:1024 + 64 * g + 64]
                vc = w_qkv[:, 1280 + 64 * g:1280 + 64 * g + 64]
                cols = np.concatenate([w_qkv[:, 256 * g:256 * g + 256], kc, kc, vc, vc], axis=1)
                pieces.append(("swa%d" % g, np.concatenate([_fm(cols), _fm(w_out[256 * g:256 * g + 256, :])], axis=1)))
        wkv = inp["xa_wkv"][l]
        pieces.append(("xk", _fm(wkv[:, 0:1024])))
        pieces.append(("xv", _fm(wkv[:, 1024:2048])))
        wq = inp["xa_wq"][l]
        wo = inp["xa_wo"][l]
        for j in range(2):
            pieces.append(("xq%d" % j, np.concatenate([_fm(wq[:, 512 * j:512 * j + 512]), _fm(wo[512 * j:512 * j + 512, :])], axis=1)))
        wgu = inp["ffn_w_gate_up"][l]
        wd = inp["ffn_w_down"][l]
        for (c0, n) in FFN_GROUPS:
            cols = np.concatenate([wgu[:, 128 * c0:128 * (c0 + n)], wgu[:, FH + 128 * c0:FH + 128 * (c0 + n)]], axis=1)
            pieces.append(("fgu", _fm(cols)))
            pieces.append(("fd", _fm(wd[128 * c0:128 * (c0 + n), :])))
    return pieces


def build_small(inp):
    sm = np.zeros((128, NSM), np.float32)
    p = np.arange(128)
    for l in range(2):
        sm[:, SM_MIX + 8 * l:SM_MIX + 8 * l + 8] = inp["mix_norm"][l].reshape(8, 128).T
        sm[:, SM_XAN + 8 * l:SM_XAN + 8 * l + 8] = inp["xa_norm"][l].reshape(8, 128).T
        sm[:, SM_FFN + 8 * l:SM_FFN + 8 * l + 8] = inp["ffn_norm"][l].reshape(8, 128).T
        sm[:, SM_MEMN + 8 * l:SM_MEMN + 8 * l + 8] = inp["mem_norm"][l].reshape(8, 128).T
        sm[:, SM_XQG + 2 * l:SM_XQG + 2 * l + 2] = inp["xa_q_gain"][l].reshape(2, 128).T
        sm[:, SM_XKG + 2 * l:SM_XKG + 2 * l + 2] = inp["xa_k_gain"][l].reshape(2, 128).T
    sm[:, SM_GBIAS:SM_GBIAS + 2] = inp["gla_gate_bias"][0].reshape(2, 128).T
    sm[:, SM_OGAIN] = inp["gla_out_gain"][0]
    cw = inp["conv_w"][0]
    for j in range(4):
        for tap in range(3):
            sm[:, SM_CONVW + 3 * j + tap] = cw[tap, 128 * j:128 * j + 128]
    sm[:, SM_SQG] = inp["swa_q_gain"][0][p % 64]
    sm[:, SM_SKG] = inp["swa_k_gain"][0][p % 64]
    sm[:, SM_SINK:SM_SINK + 16] = inp["swa_sinks"][0][None, :]
    d = p % 64
    invf = np.zeros(128, np.float32)
    theta = np.float32(500000.0)
    fr = (theta ** (-np.arange(0, 16, 2, dtype=np.float32) / np.float32(16))).astype(np.float32)
    invf[d < 16] = fr[d[d < 16] % 8]
    sm[:, SM_INVF] = invf
    return sm


def build_cmat():
    cm = np.zeros((128, 6, 128), np.float32)
    j = np.arange(128)[:, None]
    i = np.arange(128)[None, :]
    cm[:, 0, :] = (i == j)
    cm[:, 1, :] = (i >= j)
    cm[:, 2, :] = (j > i)
    pm = np.zeros((128, 128), np.float32)
    for base in (0, 64):
        for dd in range(8):
            pm[base + dd, base + dd + 8] = -1.0
            pm[base + dd + 8, base + dd] = 1.0
    cm[:, 3, :] = pm.T
    cm[:, 4, :] = np.where(i >= j, 0.0, -30000.0)
    cm[:, 5, :] = np.where(j > i, 0.0, -30000.0)
    return cm.reshape(128, 768)


class Arena:
    def __init__(self, t, n):
        self.t, self.n, self.o = t, n, 0

    def reset(self):
        self.o = 0

    def take(self, n):
        assert self.o + n <= self.n, (self.o, n, self.n)
        v = self.t[:, self.o:self.o + n]
        self.o += n
        return v


def build_program(T, NSEQ, layers=(0, 1), dbg=False, maxstage=10 ** 9):
    NT = T // TT
    NB = T // 128
    nc = bass.Bass("TRN2", target_bir_lowering=False)
    keep = [i for i in range(len(PIECE_SIZES)) if PIECE_LAYER[i] in layers]
    piece_sizes = [PIECE_SIZES[i] for i in keep]
    WTOT = sum(piece_sizes)
    xT = nc.dram_tensor("xT", [NSEQ, 128, 8 * T], F32, kind="ExternalInput").ap()
    memT = nc.dram_tensor("memT", [NSEQ, 128, 8 * MEM], F32, kind="ExternalInput").ap()
    posd = nc.dram_tensor("pos", [NSEQ, 1, T], I32, kind="ExternalInput").ap()
    wbig = nc.dram_tensor("wbig", [128, WTOT], F32, kind="ExternalInput").ap()
    smalld = nc.dram_tensor("small", [128, NSM], F32, kind="ExternalInput").ap()
    cmatd = nc.dram_tensor("cmat", [128, 768], F32, kind="ExternalInput").ap()
    yT = nc.dram_tensor("yT", [NSEQ, 128, 8 * T], F32, kind="ExternalOutput").ap()
    NDBG = 6
    if dbg:
        dbgd = nc.dram_tensor("dbg", [NDBG, 128, 8 * T], F32, kind="ExternalOutput").ap()

    es = ExitStack()
    with es:
        def sb(name, shape, dt):
            return es.enter_context(nc.sbuf_tensor(name, shape, dt))

        X = sb("X", [128, 8, T], F32)
        H = sb("H", [128, 8, T], BF16)
        W = sb("W", [128, NSLOT, SLOT], BF16)
        STG = sb("STG", [128, 2, 1024], F32)
        cm = sb("cm", [128, 6, 128], BF16)
        ones = sb("ones", [128, 128], BF16)
        bd = sb("bd", [128, 128], BF16)
        sm = sb("sm", [128, NSM], F32)
        negb = sb("negb", [128, 2], F32)
        esink = sb("esink", [128, 16], F32)
        msk = sb("msk", [128, 512], F32)
        posi = sb("posi", [128, 512], I32)
        nsq = sb("nsq", [128, 2, 512], BF16)
        nt1 = sb("nt1", [128, 512], F32)
        nrstd = sb("nrstd", [128, 512], F32)
        NSF, NSB = 4608, 11264
        SFt = sb("SF", [128, NSF], F32)
        SBt = sb("SB", [128, NSB], BF16)
        ps = es.enter_context(nc.psum_tensor("ps", [128, 8, 512], F32))
        P = Prog(nc, same_engine_sync=SAME_ENGINE_SYNC)
        for e in ("pe", "act", "dve", "pool"):
            P.add_sem(e, es.enter_context(nc.semaphore("s_" + e)))
        dkeys = ["ldx", "ldm", "ldp", "ldc", "st", "dbg"] + ["lx%d" % i for i in range(8)] + ["sx%d" % i for i in range(8)] + ["w%d" % i for i in range(NSLOT)]
        for k in dkeys:
            P.add_sem(k, es.enter_context(nc.semaphore("d_" + k)))
        SF = Arena(SFt, NSF)
        SB = Arena(SBt, NSB)

        ident, tri, trip, pmT = cm[:, 0, :], cm[:, 1, :], cm[:, 2, :], cm[:, 3, :]

        def bank(b):
            return ps[:, b, :]

        def bank2(b):
            return ps[:, b:b + 2, :]

        offs = np.cumsum([0] + piece_sizes).tolist()
        npieces_pass = len(piece_sizes)
        wst = {"loaded": 0, "cur": 0, "stg": 0}
        total_pieces = npieces_pass * NSEQ
        def piece_list():
            lst = []
            for s in range(NSEQ):
                for i in range(npieces_pass):
                    lst.append(i)
            return lst

        plist = piece_list()

        def wload_upto(n):
            while wst["loaded"] < min(n, len(plist)):
                i = wst["loaded"]
                pi_ = plist[i]
                slot = i % NSLOT
                size = piece_sizes[pi_]
                o0 = 0
                while o0 < size:
                    n_ = min(1024, size - o0)
                    j = wst["stg"] % 2
                    wst["stg"] += 1
                    P.dma("sp", STG[:, j, 0:n_], wbig[:, offs[pi_] + o0:offs[pi_] + o0 + n_], "w%d" % j)
                    P.copy("pool", W[:, slot, o0:o0 + n_], STG[:, j, 0:n_])
                    o0 += n_
                wst["loaded"] += 1

        def wacquire(k=1):
            i = wst["cur"]
            wload_upto(i + k)
            views = [W[:, (i + j) % NSLOT, :] for j in range(k)]
            wst["cur"] += k
            return views

        def wrelease():
            wload_upto(wst["cur"] + NSLOT)

        P.dma("sp", sm[:], smalld, "ldc", group_final=4)
        cmf = SF.take(768)
        P.dma("sp", cmf, cmatd, "ldc", group_final=3)
        P.dma("sp", posi[:, 0:16], posd[0, :, 0:16].partition_broadcast(128), "ldc", group_final=2)
        P.dma("sp", nt1[:, 0:16], memT[0, :, 0:16], "ldc", group_final=1)
        P.copy("dve", cm[:].rearrange("p a b -> p (a b)"), cmf)
        P.memset("dve", ones[:], 1.0)
        P.memset("dve", bd[:], 0.0)
        P.memset("dve", bd[0:64, 0:64], 1.0)
        P.memset("dve", bd[64:128, 64:128], 1.0)
        P.memset("dve", msk[:], 1.0)
        P.memset("dve", msk[:].rearrange("p (c j) -> p c j", j=128)[:, :, 0:1], 0.0)
        P.ts("dve", negb[:], sm[:, SM_GBIAS:SM_GBIAS + 2], -1.0, ALU.mult)
        P.act(esink[:], sm[:, SM_SINK:SM_SINK + 16], AF.Exp)

        def rsqrt_to(dst, src_ps, inv_n, tmp):
            P.act(tmp, src_ps, AF.Ln, bias=EPS, scale=inv_n)
            P.act(dst, tmp, AF.Exp, scale=-0.5)

        def norm_tile(gcol, t, bk=7):
            tok = slice(t * TT, (t + 1) * TT)
            ssp = bank(bk)
            for q4 in range(4):
                P.act(nsq[:], X[:, 2 * q4:2 * q4 + 2, tok], AF.Square)

                def fn(eng, q4=q4, ssp=ssp):
                    ins = None
                    for k in range(2):
                        ins = eng.matmul(ssp, ones[:], nsq[:, k, :], start=(q4 == 0 and k == 0), stop=(q4 == 3 and k == 1))
                    return ins
                P.op("pe", fn, [ones[:], nsq[:]], [ssp])
            rsqrt_to(nrstd[:], ssp, 1.0 / D, nt1[:])
            for k in range(8):
                P.stt(H[:, k, tok], X[:, k, tok], sm[:, gcol + k:gcol + k + 1], nrstd[:], ALU.mult, ALU.mult)

        def norm_phase(gcol):
            for t in range(NT):
                norm_tile(gcol, t, 6 + t % 2)

        def out_proj_add(wo, nk, y, tok, banks=(0, 1, 2, 3, 4, 5, 6, 7)):
            for oc in range(8):
                pb = bank(banks[oc % len(banks)])
                P.mm(pb, [(wo[:, k, oc * 128:(oc + 1) * 128], y[:, k, :]) for k in range(nk)])
                P.tt("dve", X[:, oc, tok], X[:, oc, tok], pb, ALU.add)

        def proj_fm(pb, wv, c0, m, tok):
            P.mm(pb[0:m, :] if m < 128 else pb, [(wv[:, k, c0:c0 + m], H[:, k, tok]) for k in range(8)])

        def lane(*gens):
            for g_ in gens:
                yield from g_

        def rr(*gens):
            gens = list(gens)
            while gens:
                for g_ in list(gens):
                    try:
                        next(g_)
                    except StopIteration:
                        gens.remove(g_)

        def stage_gla_all():
            wviews = {}
            base = wst["cur"]

            def getw(c):
                if c not in wviews:
                    assert wst["cur"] == base + c
                    (v,) = wacquire(1)
                    wviews[c] = (v[:, 0:6272].rearrange("p (k n) -> p k n", k=8),
                                 v[:, 6272:8320].rearrange("p (k n) -> p k n", k=2),
                                 v[0:16, 8320:8448])
                return wviews[c]
            SF.reset(); SB.reset()
            sp_ = SF.take(512); nb = SF.take(512); ea = SF.take(512); eb_ = SF.take(512); tmp = SF.take(512)
            t1 = SF.take(512); orstd = SF.take(512); yt = SF.take(512)
            S = SF.take(128)
            decs = [SF.take(128) for _ in range(2)]
            bufs = []
            for _ in range(2):
                bufs.append(dict(q_in=SB.take(512), k_in=SB.take(512),
                                 k_ot=SB.take(512).rearrange("p (b n) -> p b n", b=4),
                                 v_tok=SB.take(1024).rearrange("p (b n) -> p b n", b=4),
                                 silg=SB.take(1024).rearrange("p (h n) -> p h n", h=2)))
            k_oT = SB.take(512)
            glow = k_oT
            osq_flat = SB.take(1024)
            osq = osq_flat.rearrange("p (h n) -> p h n", h=2)
            sT_all = osq_flat.rearrange("p (h b n) -> p h b n", h=2, b=4)
            ygla = SB.take(1024).rearrange("p (h n) -> p h n", h=2)
            S_bfa = SB.take(512).rearrange("p (b n) -> p b n", b=4)
            nb3 = nb.rearrange("p (b n) -> p b n", b=4)
            tmp3 = tmp.rearrange("p (b n) -> p b n", b=4)

            def prep(c, t):
                win, wo, wg2 = getw(c)
                tok = slice(t * TT, (t + 1) * TT)
                B_ = bufs[t % 2]
                q_in, k_in, k_ot, v_tok, silg = B_["q_in"], B_["k_in"], B_["k_ot"], B_["v_tok"], B_["silg"]
                dec = decs[t % 2]
                glp = bank(3)
                proj_fm(glp, win, 512, 16, tok)
                yield
                P.copy("act", glow[0:16, :], glp[0:16, :])
                yield
                zp = bank(4)
                P.mm(zp, [(wg2, glow[0:16, :])])
                yield
                P.act(ea, zp, AF.Exp, bias=negb[:, c:c + 1], scale=-1.0)
                P.act(sp_, ea, AF.Ln, bias=1.0)
                yield
                P.scan(nb, msk[:], sp_, 0.0, ALU.mult, ALU.add)
                qp = bank(3); proj_fm(qp, win, 0, 128, tok)
                kp = bank(4); proj_fm(kp, win, 128, 128, tok)
                yield
                P.act(dec[:, 0:4], nb3[:, :, 127], AF.Exp, scale=-1.0 / 16)
                P.act(ea, nb, AF.Exp, scale=-1.0 / 16)
                P.act(eb_, nb, AF.Exp, scale=1.0 / 16)
                yield
                P.stt(q_in, qp, 0.125, ea, ALU.mult, ALU.mult)
                P.tt("dve", k_in, kp, eb_, ALU.mult)
                P.tt("dve", tmp3, nb3, nb3[:, :, 127:128].to_broadcast([128, 4, 128]), ALU.subtract)
                yield
                P.act(ea, tmp, AF.Exp, scale=1.0 / 16)
                yield
                P.tt("dve", k_oT, kp, ea, ALU.mult)
                yield
                tpb = bank(5)
                tp3 = tpb.rearrange("p (b n) -> p b n", b=4)
                P.mm_multi([(tp3[:, b, :], [(k_oT[:, b * 128:(b + 1) * 128], ident)]) for b in range(4)])
                yield
                P.copy("act", k_ot, tp3)
                yield
                for b in range(4):
                    vb = bank(3 if b % 2 else 5)
                    P.mm(vb[:, 0:256], [(H[:, k, t * TT + b * 128:t * TT + (b + 1) * 128], win[:, k, 256:512]) for k in range(8)])
                    yield
                    P.copy("act", v_tok[:, b, :], vb[:, 0:256])
                    yield
                for hl in range(2):
                    gp_ = bank(3 + hl)
                    proj_fm(gp_, win, 528 + 128 * hl, 128, tok)
                    yield
                    P.act(silg[:, hl, :], gp_, AF.Silu)
                    yield

            def blocks(c, t):
                win, wo, wg2 = getw(c)
                if t == 0:
                    P.memset("dve", S, 0.0)
                tok = slice(t * TT, (t + 1) * TT)
                B_ = bufs[t % 2]
                q_in, k_in, k_ot, v_tok, silg = B_["q_in"], B_["k_in"], B_["k_ot"], B_["v_tok"], B_["silg"]
                dec = decs[t % 2]
                op2 = bank2(6)
                kv4 = ps[:, 0:2, :].rearrange("p a (b n) -> p (a b) n", b=2)
                P.mm_multi([(kv4[:, b, :], [(k_ot[:, b, :], v_tok[:, b, :])]) for b in range(4)])
                yield
                for b in range(4):
                    P.copy("dve", S_bfa[:, b, :], S)
                    for hl in range(2):
                        pr = slice(64 * hl, 64 * hl + 64)
                        P.stt(S[pr, :], S[pr, :], dec[pr, b:b + 1], kv4[pr, b, hl * 128:(hl + 1) * 128], ALU.mult, ALU.add)
                yield
                sc4 = ps[:, 0:2, :].rearrange("p h (b n) -> p h b n", b=4)
                P.mm_multi([(sc4[:, hl, b, :], [(k_in[64 * hl:64 * hl + 64, b * 128:(b + 1) * 128],
                                                 q_in[64 * hl:64 * hl + 64, b * 128:(b + 1) * 128])])
                            for b in range(4) for hl in range(2)])
                yield
                P.tt("dve", sT_all, sc4, tri.unsqueeze(1).unsqueeze(1).to_broadcast([128, 2, 4, 128]), ALU.mult)
                yield
                P.mm_multi([(op2[:, hl, b * 128:(b + 1) * 128],
                             [(v_tok[:, b, hl * 128:(hl + 1) * 128], sT_all[:, hl, b, :]),
                              (S_bfa[64 * hl:64 * hl + 64, b, :], q_in[64 * hl:64 * hl + 64, b * 128:(b + 1) * 128])])
                            for b in range(4) for hl in range(2)])
                yield
                P.act(osq, op2, AF.Square)
                yield
                for hl in range(2):
                    ssp = bank(hl)
                    P.mm(ssp, [(ones[:], osq[:, hl, :])])
                    yield
                    rsqrt_to(orstd, ssp, 1.0 / 128, t1)
                    yield
                    P.stt(yt, op2[:, hl, :], sm[:, SM_OGAIN:SM_OGAIN + 1], orstd, ALU.mult, ALU.mult)
                    P.tt("dve", ygla[:, hl, :], yt, silg[:, hl, :], ALU.mult)
                    yield
                obanks = (0, 1, 2, 6, 7)
                for oc in range(8):
                    pb = bank(obanks[oc % 5])
                    P.mm(pb, [(wo[:, k, oc * 128:(oc + 1) * 128], ygla[:, k, :]) for k in range(2)])
                    yield
                    P.tt("dve", X[:, oc, tok], X[:, oc, tok], pb, ALU.add)
                    yield

            rr(prep(0, 0))
            for c in range(2):
                for t in range(NT):
                    nxt = (c, t + 1) if t + 1 < NT else ((c + 1, 0) if c + 1 < 2 else None)
                    if nxt is not None:
                        rr(blocks(c, t), prep(*nxt))
                    else:
                        rr(blocks(c, t))
                wload_upto(base + c + 1 + NSLOT)

        def stage_conv_all(post=None):
            wviews = {}
            base = wst["cur"]

            def getw(c):
                if c not in wviews:
                    assert wst["cur"] == base + c
                    (v,) = wacquire(1)
                    wviews[c] = (v[:, 0:6144].rearrange("p (k n) -> p k n", k=8),
                                 v[:, 6144:8192].rearrange("p (k n) -> p k n", k=2))
                return wviews[c]
            SF.reset(); SB.reset()
            ccs = SF.take(1024).rearrange("p (j n) -> p j n", j=2)
            U = SF.take(2 * 516).rearrange("p (j n) -> p j n", j=2)
            acc = SF.take(1024).rearrange("p (j n) -> p j n", j=2)
            ycs = [SB.take(1024).rearrange("p (j n) -> p j n", j=2) for _ in range(2)]

            def chain(c, t, j):
                win, wo = getw(c)
                if t == 0:
                    P.memset("dve", U[:, j, 0:2], 0.0)
                tok = slice(t * TT, (t + 1) * TT)
                cb = bank(j); cc = bank(2 + j); ci = bank(4 + j)
                proj_fm(cc, win, 256 + 128 * j, 128, tok)
                proj_fm(ci, win, 512 + 128 * j, 128, tok)
                yield
                P.copy("act", ccs[:, j, :], cc)
                proj_fm(cb, win, 128 * j, 128, tok)
                yield
                P.tt("dve", U[:, j, 2:514], ci, ccs[:, j, :], ALU.mult)
                yield
                cw = SM_CONVW + 3 * (2 * c + j)
                P.ts("dve", acc[:, j, :], U[:, j, 2:514], sm[:, cw + 2:cw + 3], ALU.mult)
                P.stt(acc[:, j, :], U[:, j, 1:513], sm[:, cw + 1:cw + 2], acc[:, j, :], ALU.mult, ALU.add)
                P.stt(acc[:, j, :], U[:, j, 0:512], sm[:, cw:cw + 1], acc[:, j, :], ALU.mult, ALU.add)
                yield
                P.tt("dve", ycs[t % 2][:, j, :], cb, acc[:, j, :], ALU.mult)
                P.copy("dve", U[:, j, 0:2], U[:, j, 512:514])
                yield

            def outl(c, t):
                win, wo = getw(c)
                tok = slice(t * TT, (t + 1) * TT)
                for oc in range(8):
                    pb = bank(6 + oc % 2)
                    P.mm(pb, [(wo[:, k, oc * 128:(oc + 1) * 128], ycs[t % 2][:, k, :]) for k in range(2)])
                    yield
                    P.tt("dve", X[:, oc, tok], X[:, oc, tok], pb, ALU.add)
                    yield
                if post is not None and c == 1:
                    norm_tile(post, t, 7)
                    yield

            rr(chain(0, 0, 0), chain(0, 0, 1))
            for c in range(2):
                for t in range(NT):
                    nxt = (c, t + 1) if t + 1 < NT else ((c + 1, 0) if c + 1 < 2 else None)
                    if nxt is not None:
                        rr(chain(nxt[0], nxt[1], 0), chain(nxt[0], nxt[1], 1), outl(c, t))
                    else:
                        rr(outl(c, t))
                wload_upto(base + c + 1 + NSLOT)

        def stage_xa(l, s, post=None):
            SF.reset(); SB.reset()
            kT = SB.take(2048).rearrange("p (c n) -> p c n", c=8)
            vtk = SB.take(2048).rearrange("p (b n) -> p b n", b=2)
            sb_mark = SB.o
            mT = SF.take(2048).rearrange("p (k n) -> p k n", k=8)
            t1 = SF.take(512); rstd = SF.take(512); rden = SF.take(512)
            sqm = SB.take(2048).rearrange("p (k n) -> p k n", k=8)
            Hm = SB.take(2048).rearrange("p (k n) -> p k n", k=8)
            P.dma("sp", mT.rearrange("p k n -> p (k n)"), memT[s], "ldm")
            P.act(sqm, mT, AF.Square)
            ssp = bank(7)
            P.mm(ssp[:, 0:256], [(ones[:], sqm[:, k, :]) for k in range(8)])
            rsqrt_to(rstd[:, 0:256], ssp[:, 0:256], 1.0 / D, t1[:, 0:256])
            for k in range(8):
                P.stt(Hm[:, k, :], mT[:, k, :], sm[:, SM_MEMN + 8 * l + k:SM_MEMN + 8 * l + k + 1], rstd[:, 0:256], ALU.mult, ALU.mult)
            (v,) = wacquire(1)
            wk = v[:, 0:8192].rearrange("p (k n) -> p k n", k=8)
            sqk = sqm[:, 0:2, :]
            for h in range(4):
                kp = bank(h % 2)
                kp2 = kp.rearrange("p (c n) -> p c n", c=2)
                for c2 in range(2):
                    oc = 2 * h + c2
                    P.mm(kp2[:, c2, :], [(wk[:, k, oc * 128:(oc + 1) * 128], Hm[:, k, :]) for k in range(8)])
                P.act(sqk, kp2, AF.Square)
                ssp = bank(2 + h % 2)
                P.mm(ssp[:, 0:256], [(ones[:], sqk[:, c2, :]) for c2 in range(2)])
                rsqrt_to(rstd[:, 0:256], ssp[:, 0:256], 1.0 / 256, t1[:, 0:256])
                for c2 in range(2):
                    P.stt(kT[:, 2 * h + c2, :], kp2[:, c2, :], sm[:, SM_XKG + 2 * l + c2:SM_XKG + 2 * l + c2 + 1], rstd[:, 0:256], ALU.mult, ALU.mult)
            wrelease()
            (v,) = wacquire(1)
            wv = v[:, 0:8192].rearrange("p (k n) -> p k n", k=8)
            for kb in range(2):
                for hf in range(2):
                    vb = bank(4 + 2 * kb + hf)
                    P.mm(vb, [(Hm[:, k, kb * 128:(kb + 1) * 128], wv[:, k, hf * 512:(hf + 1) * 512]) for k in range(8)])
                    P.copy("act", vtk[:, kb, hf * 512:(hf + 1) * 512], vb)
            wrelease()
            SB.o = sb_mark
            SF.reset()
            sqps = [SB.take(1024).rearrange("p (c n) -> p c n", c=2) for _ in range(2)]
            qns = [SB.take(1024).rearrange("p (c n) -> p c n", c=2) for _ in range(2)]
            oT = SB.take(2048).rearrange("p (c n) -> p c n", c=4)
            t1s = [SF.take(512) for _ in range(2)]
            rstds = [SF.take(512) for _ in range(2)]
            rdens = [SF.take(512) for _ in range(2)]
            for j in range(2):
                (v,) = wacquire(1)
                wq = v[:, 0:4096].rearrange("p (k n) -> p k n", k=8)
                wo = v[:, 4096:8192].rearrange("p (k n) -> p k n", k=4)
                prev_tok = None
                for t in range(NT):
                    tok = slice(t * TT, (t + 1) * TT)
                    qps, ssps = [], []
                    for hl in range(2):
                        qp = bank2(4 * hl)
                        for c2 in range(2):
                            proj_fm(qp[:, c2, :], wq, hl * 256 + c2 * 128, 128, tok)
                        P.act(sqps[hl], qp, AF.Square)
                        ssp = bank(4 * hl + 2)
                        P.mm(ssp, [(ones[:], sqps[hl][:, c2, :]) for c2 in range(2)])
                        qps.append(qp); ssps.append(ssp)
                    if prev_tok is not None:
                        out_proj_add(wo, 4, oT, prev_tok, banks=(3, 7))
                        if post is not None and j == 1:
                            norm_tile(post, t - 1, 3)
                    for hl in range(2):
                        rsqrt_to(rstds[hl], ssps[hl], 1.0 / 256, t1s[hl])
                        for c2 in range(2):
                            P.stt(qns[hl][:, c2, :], qps[hl][:, c2, :], sm[:, SM_XQG + 2 * l + c2:SM_XQG + 2 * l + c2 + 1], rstds[hl], ALU.mult, ALU.mult)
                    for hl in range(2):
                        h = 2 * j + hl
                        sp2 = bank2(4 * hl + 2)
                        for kb in range(2):
                            P.mm(sp2[:, kb, :], [(kT[:, 2 * h + c2, kb * 128:(kb + 1) * 128], qns[hl][:, c2, :]) for c2 in range(2)])
                        P.act(sqps[hl], sp2, AF.Exp, scale=1.0 / 16)
                    for hl in range(2):
                        h = 2 * j + hl
                        dp = bank(4 * hl + 2)
                        P.mm(dp, [(ones[:], sqps[hl][:, kb, :]) for kb in range(2)])
                        P.act(t1s[hl], dp, AF.Ln)
                        P.act(rdens[hl], t1s[hl], AF.Exp, scale=-1.0)
                        op2 = bank2(4 * hl)
                        for c2 in range(2):
                            P.mm(op2[:, c2, :], [(vtk[:, kb, h * 256 + c2 * 128:h * 256 + (c2 + 1) * 128], sqps[hl][:, kb, :]) for kb in range(2)])
                        for c2 in range(2):
                            P.tt("dve", oT[:, hl * 2 + c2, :], op2[:, c2, :], rdens[hl], ALU.mult)
                    prev_tok = tok
                out_proj_add(wo, 4, oT, prev_tok)
                if post is not None and j == 1:
                    norm_tile(post, NT - 1, 7)
                wrelease()

        def stage_ffn(l, post=None, defer_release=False):
            SF.reset(); SB.reset()
            sgs = [SF.take(512) for _ in range(2)]
            a = SB.take(2048).rearrange("p (c n) -> p c n", c=4)
            for (c0, n) in FFN_GROUPS:
                va, vb_ = wacquire(2)
                wgu = va[:, 0:8 * 256 * n].rearrange("p (k n) -> p k n", k=8)
                wd = vb_[:, 0:1024 * n].rearrange("p (k n) -> p k n", k=n)
                for t in range(NT):
                    tok = slice(t * TT, (t + 1) * TT)
                    for ch in range(n):
                        gp = bank(2 * (ch % 2)); proj_fm(gp, wgu, ch * 128, 128, tok)
                        up = bank(2 * (ch % 2) + 1); proj_fm(up, wgu, n * 128 + ch * 128, 128, tok)
                        sg = sgs[ch % 2]
                        P.act(sg, gp, AF.Silu)
                        P.tt("dve", a[:, ch, :], up, sg, ALU.mult)
                    out_proj_add(wd, n, a, tok, banks=(4, 5, 6, 7))
                    if post is not None and c0 == FFN_GROUPS[-1][0]:
                        norm_tile(post, t, 7)
                if not (defer_release and c0 == FFN_GROUPS[-1][0]):
                    wrelease()

        def stage_swa_all(s, post=None):
            SF.reset(); SB.reset()
            Ct = SF.take(512); St = SF.take(512)
            f = [SF.take(512) for _ in range(6)]
            pif = SF.take(512)
            pi_ = posi[:]
            qrots = [SB.take(1024).rearrange("p (c n) -> p c n", c=2) for _ in range(2)]
            krot = SB.take(T)
            vvt = SB.take(NB * 128).rearrange("p (b n) -> p b n", b=NB)
            sq = SB.take(512); qn = SB.take(512)
            pTs = [SB.take(1024).rearrange("p (h w c n) -> p h w c n", h=2, w=2, c=2) for _ in range(2)]
            yT_ = SB.take(1024).rearrange("p (c n) -> p c n", c=2)
            esrs = [SB.take(512) for _ in range(2)]
            wviews = {}
            base = wst["cur"]

            def getw(g):
                if g not in wviews:
                    assert wst["cur"] == base + g
                    (v,) = wacquire(1)
                    wviews[g] = (v[:, 0:4096].rearrange("p (k n) -> p k n", k=8),
                                 v[:, 4096:6144].rearrange("p (k n) -> p k n", k=2))
                    esr = esrs[g % 2]
                    for hf in range(2):
                        for c_ in range(2):
                            o_ = (2 * hf + c_) * 128
                            P.copy("dve", esr[0:1, o_:o_ + 128], esink[0:1, 4 * g + 2 * c_ + hf:4 * g + 2 * c_ + hf + 1].to_broadcast([1, 128]))
                return wviews[g]

            def tables(t):
                tok = slice(t * TT, (t + 1) * TT)
                A, B = f[4], f[5]
                P.dma("sp", pi_, posd[s, :, tok].partition_broadcast(128), "ldp")
                P.copy("dve", A, pi_)
                P.ts("dve", A, A, sm[:, SM_INVF:SM_INVF + 1], ALU.mult)
                yield
                for which, dst in ((0, St), (1, Ct)):
                    if which == 1:
                        P.ts("dve", A, A, PI / 2, ALU.add)
                    P.ts("dve", B, A, 1.0 / TWO_PI, ALU.mult)
                    P.copy("dve", pi_, B)
                    P.copy("dve", B, pi_)
                    yield
                    P.stt(B, B, -TWO_PI, A, ALU.mult, ALU.add)
                    P.ts("dve", pif, B, PI, ALU.is_gt, s2=TWO_PI, op1=ALU.mult)
                    P.tt("dve", B, B, pif, ALU.subtract)
                    yield
                    P.act(dst, B, AF.Sin)
                    yield

            def chunk(g, t, ci):
                win, wo = getw(g)
                tok = slice(t * TT, (t + 1) * TT)
                pb = bank(4)
                proj_fm(pb, win, 128 * ci, 128, tok)
                yield
                P.act(sq, pb, AF.Square)
                yield
                ssp = bank(5)
                P.mm(ssp, [(bd[:], sq)])
                yield
                P.act(pif, ssp, AF.Ln, bias=EPS, scale=1.0 / 64)
                P.act(f[4], pif, AF.Exp, scale=-0.5)
                yield
                gcol = SM_SQG if ci < 2 else SM_SKG
                P.stt(qn, pb, sm[:, gcol:gcol + 1], f[4], ALU.mult, ALU.mult)
                yield
                pq = bank(6)
                P.mm(pq, [(pmT, qn)])
                P.tt("dve", f[4], qn, Ct, ALU.mult)
                yield
                P.tt("dve", f[5], pq, St, ALU.mult)
                dst = qrots[t % 2][:, ci, :] if ci < 2 else krot[:, tok]
                P.tt("dve", dst, f[4], f[5], ALU.add)
                yield

            def vproj(g, t):
                win, wo = getw(g)
                vb = bank(7)
                vb3 = vb.rearrange("p (b n) -> p b n", b=4)
                P.mm_multi([(vb3[:, b, :], [(H[:, k, t * TT + b * 128:t * TT + (b + 1) * 128], win[:, k, 384:512]) for k in range(8)]) for b in range(4)])
                yield
                P.copy("act", vvt[:, 4 * t:4 * t + 4, :], vb3)
                yield

            def block(g, t, b):
                esr = esrs[g % 2]
                n = 4 * t + b
                bt = slice(b * 128, (b + 1) * 128)
                p_ = b % 2
                nw = 2 if n > 0 else 1
                qrot = qrots[t % 2]
                pT = pTs[p_]
                fa, fb = f[2 * p_], f[2 * p_ + 1]
                sc = [bank(2 * p_ + hf).rearrange("p (w c n) -> p w c n", w=2, c=2) for hf in range(2)]
                items = []
                for w_ in range(nw):
                    kb_ = n - w_
                    for c_ in range(2):
                        for hf in range(2):
                            pr = slice(64 * hf, 64 * hf + 64)
                            items.append((sc[hf][:, w_, c_, :], (ident, cm[:, 4 + w_, :]),
                                          (krot[pr, kb_ * 128:(kb_ + 1) * 128], qrot[pr, c_, bt])))

                def fn(eng, items=items):
                    ins = None
                    ni = len(items)
                    for i, (o_, (ml, mr), _) in enumerate(items):
                        ins = eng.matmul(o_, ml, mr, start=(i < 2), stop=False)
                    for i, (o_, _, (sl, sr)) in enumerate(items):
                        ins = eng.matmul(o_, sl, sr, start=False, stop=(i >= ni - 2))
                    return ins
                rd_ = []
                for o_, (ml, mr), (sl, sr) in items:
                    rd_ += [ml, mr, sl, sr]
                P.op("pe", fn, rd_, [it[0] for it in items])
                yield
                for hf in range(2):
                    P.act(pT[:, hf, 0:nw], sc[hf][:, 0:nw], AF.Exp, scale=0.125)
                yield
                db = bank(2 * p_); ob = bank(2 * p_ + 1)
                ob4 = ob.rearrange("p (h n) -> p h n", h=4)
                grp = []
                for hl in range(4):
                    c_, hf = hl // 2, hl % 2
                    grp.append((ob4[:, hl, :], [(vvt[:, n - w_, :], pT[:, hf, w_, c_, :]) for w_ in range(nw)]))
                P.mm_multi(grp)
                db3 = db.rearrange("p (h m) -> p h m", h=2)
                P.mm_multi([(db3[:, hf, :], [(ones[:], pT[:, hf, w_].rearrange("p c n -> p (c n)")) for w_ in range(nw)]
                             + [(ones[0:1, :], esr[0:1, hf * 256:(hf + 1) * 256])]) for hf in range(2)])
                yield
                P.act(fa, db, AF.Ln)
                P.act(fb, fa, AF.Exp, scale=-1.0)
                yield
                rd = fb.rearrange("p (h c n) -> p h c n", h=2, c=2)
                ob5 = ob.rearrange("p (c h n) -> p c h n", c=2, h=2)
                for hf in range(2):
                    pr = slice(64 * hf, 64 * hf + 64)
                    P.tt("dve", yT_[pr, :, bt], ob5[pr, :, hf, :], rd[pr, hf, :, :], ALU.mult)
                yield

            def outl(g, t):
                win, wo = getw(g)
                tok = slice(t * TT, (t + 1) * TT)
                obanks = (4, 5, 6, 7) if (g == 3 and t == NT - 1) else (0, 1, 2, 3)
                for oc in range(8):
                    pb = bank(obanks[oc % 4])
                    P.mm(pb, [(wo[:, k, oc * 128:(oc + 1) * 128], yT_[:, k, :]) for k in range(2)])
                    yield
                    P.tt("dve", X[:, oc, tok], X[:, oc, tok], pb, ALU.add)
                    yield
                if post is not None and g == 3:
                    norm_tile(post, t, 7 if t == NT - 1 else 3)
                    yield

            def prep(g, t):
                return lane(tables(t), chunk(g, t, 0), chunk(g, t, 1), chunk(g, t, 2), vproj(g, t))

            rr(prep(0, 0))
            for g in range(4):
                for t in range(NT):
                    la = lane(block(g, t, 0), block(g, t, 1), block(g, t, 2), block(g, t, 3), outl(g, t))
                    nxt = (g, t + 1) if t + 1 < NT else ((g + 1, 0) if g + 1 < 4 else None)
                    if nxt is not None:
                        rr(la, prep(*nxt))
                    else:
                        rr(la)
                wload_upto(base + g + 1 + NSLOT)

        di = {"i": 0}

        def dump():
            if dbg and di["i"] < NDBG:
                for k in range(8):
                    P.dma("sp", dbgd[di["i"], :, k * T:(k + 1) * T], X[:, k, :], "dbg", group_final=8 - k)
                di["i"] += 1

        stc = {"i": 0}

        def run(fn, *a):
            stc["i"] += 1
            if stc["i"] <= maxstage:
                fn(*a)

        def xload(s_, t):
            tok = slice(t * TT, (t + 1) * TT)
            P.dma("sp", X[:, :, tok], xT[s_].rearrange("p (k n) -> p k n", k=8)[:, :, tok], "lx%d" % (t % 8))

        for s in range(NSEQ):
            if s == 0:
                xload(s, 0)
                wload_upto(1)
                for t in range(1, NT):
                    xload(s, t)
                wload_upto(NSLOT)
            for li, l in enumerate(layers):
                if li == 0:
                    run(norm_phase, SM_MIX + 8 * l)
                if l == 0:
                    run(stage_gla_all); run(stage_conv_all, SM_XAN + 8 * l)
                else:
                    run(stage_swa_all, s, SM_XAN + 8 * l)
                if s == 0:
                    dump()
                run(stage_xa, l, s, SM_FFN + 8 * l)
                if s == 0:
                    dump()
                nxt = layers[li + 1] if li + 1 < len(layers) else None
                run(stage_ffn, l, (SM_MIX + 8 * nxt) if nxt is not None else None, nxt is None)
                if s == 0:
                    dump()
            for t in range(NT):
                tok = slice(t * TT, (t + 1) * TT)
                P.dma("sp", yT[s].rearrange("p (k n) -> p k n", k=8)[:, :, tok], X[:, :, tok], "sx%d" % (t % 8))
                if s + 1 < NSEQ:
                    xload(s + 1, t)
            wrelease()
        P.wait_all_dma("sp", ["sx%d" % i for i in range(8)] + ["dbg"])
        P.emit()
        print("ops", P.nops, "waits", P.nwaits, {e: len(q) for e, q in P.q.items()})
    return nc


def _piece_meta():
    sizes, lay = [], []
    for l in range(2):
        if l == 0:
            sizes += [8448, 8448, 8192, 8192]; lay += [0] * 4
        else:
            sizes += [6144] * 4; lay += [1] * 4
        sizes += [8192, 8192, 8192, 8192]; lay += [l] * 4
        for (c0, n) in FFN_GROUPS:
            sizes += [8 * 256 * n, 1024 * n]; lay += [l, l]
    return sizes, lay


PIECE_SIZES, PIECE_LAYER = _piece_meta()


def prep_inputs(inp, T, nseq_per_core, ncores, seq_ids=None, layers=(0, 1)):
    pieces = build_pieces(inp)
    assert [p[1].shape[1] for p in pieces] == PIECE_SIZES, [p[1].shape[1] for p in pieces]
    pieces = [pieces[i] for i in range(len(pieces)) if PIECE_LAYER[i] in layers]
    wbig = np.ascontiguousarray(np.concatenate([p[1] for p in pieces], axis=1).astype(np.float32))
    small = build_small(inp)
    cmat = build_cmat()
    x = np.asarray(inp["x"]); mem = np.asarray(inp["mem"]); pos = np.asarray(inp["positions"])
    maps = []
    for c in range(ncores):
        ids = seq_ids[c] if seq_ids is not None else list(range(c * nseq_per_core, (c + 1) * nseq_per_core))
        xs = x[ids][:, :T, :]
        xTt = xs.transpose(0, 2, 1).reshape(len(ids), 8, 128, T).transpose(0, 2, 1, 3).reshape(len(ids), 128, 8 * T)
        ms = mem[ids]
        mTt = ms.transpose(0, 2, 1).reshape(len(ids), 8, 128, MEM).transpose(0, 2, 1, 3).reshape(len(ids), 128, 8 * MEM)
        maps.append({"xT": np.ascontiguousarray(xTt, dtype=np.float32), "memT": np.ascontiguousarray(mTt, dtype=np.float32),
                     "pos": np.ascontiguousarray(pos[ids][:, None, :T].astype(np.int32)),
                     "wbig": wbig, "small": small, "cmat": cmat})
    return maps


def unpack_out(yT, T):
    n = yT.shape[0]
    return yT.reshape(n, 128, 8, T).transpose(0, 2, 1, 3).reshape(n, 1024, T).transpose(0, 2, 1)


def kernel(**inputs):
    inp = {k: np.asarray(v) for k, v in inputs.items()}
    T = 2048
    nc = build_program(T, 2)
    maps = prep_inputs(inp, T, 2, NCORES)
    res = run_bass_kernel_spmd(nc, maps, core_ids=list(range(NCORES)))
    outs = [unpack_out(r["yT"], T) for r in res.results]
    return np.ascontiguousarray(np.concatenate(outs, axis=0).astype(np.float32))
```
